# Optimizing a Trainium2 kernel written in Bass

```python
import math
import jax, jax.numpy as jnp
from jax import lax
import numpy as np

D_MODEL = 1024
BATCH = 8
SEQ = 4096
DEPTH = 4

A_HEADS = 4
A_QK_DIM = 64
A_V_DIM = 2 * A_QK_DIM
A_WIDTH = A_HEADS * A_V_DIM
B_PAIRS = ((128, 1), (512, 4), (2048, 16))
B_GROUPS = len(B_PAIRS)
B_HEADS = 4
B_HEAD_DIM = 128
B_WIDTH = B_HEADS * B_HEAD_DIM
B_BLOCK = 128
Q_BLOCK = 128
A_Q_COLS = A_HEADS * 2 * A_QK_DIM
A_K_COLS = A_HEADS * 2 * A_QK_DIM
A_V_COLS = A_HEADS * A_V_DIM
B_COLS = B_GROUPS * B_WIDTH
GATE_COLS = 2 * D_MODEL
IN_COLS = A_Q_COLS + A_K_COLS + A_V_COLS + 3 * B_COLS + GATE_COLS
D_FF = 2816
CONV_WIDTH = 3
ROPE_THETA = 10000.0
NORM_EPS = 1e-6

kernel_name = "hybrid_diffattn_dilated_convglu"


def rms_norm(x, g):
    xf = x.astype(jnp.float32)
    y = xf * lax.rsqrt(jnp.mean(xf * xf, axis=-1, keepdims=True) + NORM_EPS)
    return (y * g.astype(jnp.float32)).astype(x.dtype)


def rope_tables(positions, dim):
    inv = ROPE_THETA ** (-jnp.arange(0, dim, 2, dtype=jnp.float32) / dim)
    ang = positions.astype(jnp.float32)[..., None] * inv
    return jnp.cos(ang)[:, :, None, None, :], jnp.sin(ang)[:, :, None, None, :]


def apply_rope(t, cos, sin):
    tf = t.astype(jnp.float32)
    t1, t2 = jnp.split(tf, 2, axis=-1)
    return jnp.concatenate([t1 * cos - t2 * sin, t2 * cos + t1 * sin], axis=-1).astype(t.dtype)


def diff_attention(q, k, v, lam):
    b, s, h, _, dqk = q.shape
    nq = s // Q_BLOCK
    scale = dqk ** -0.5
    qb = q.reshape(b, nq, Q_BLOCK, h, 2, dqk).transpose(1, 0, 2, 3, 4, 5)
    kpos = jnp.arange(s)

    def one_block(args):
        qi, i = args
        sc = jnp.einsum('bqhcd,bkhcd->bchqk', qi, k).astype(jnp.float32) * scale
        qpos = i * Q_BLOCK + jnp.arange(Q_BLOCK)
        causal = kpos[None, :] <= qpos[:, None]
        p = jax.nn.softmax(jnp.where(causal, sc, -jnp.inf), axis=-1)
        a = p[:, 0] - lam * p[:, 1]
        return jnp.einsum('bhqk,bkhd->bqhd', a.astype(v.dtype), v)

    o = lax.map(one_block, (qb, jnp.arange(nq)))
    return o.transpose(1, 0, 2, 3, 4).reshape(b, s, h, v.shape[-1])


def dilated_group(q, k, v, window, dil):
    b, s, h, d = q.shape
    L = s // dil
    nwin = window // dil
    nb = -(-L // B_BLOCK)
    Lp = nb * B_BLOCK

    def to_blocks(t):
        t = t.reshape(b, L, dil, h, d).transpose(0, 2, 1, 3, 4)
        t = jnp.pad(t, ((0, 0), (0, 0), (0, Lp - L), (0, 0), (0, 0)))
        return t.reshape(b, dil, nb, B_BLOCK, h, d)

    def with_prev(t):
        prev = jnp.pad(t[:, :, :-1], ((0, 0), (0, 0), (1, 0), (0, 0), (0, 0), (0, 0)))
        return jnp.concatenate([prev, t], axis=3)

    qb, kb, vb = to_blocks(q), to_blocks(k), to_blocks(v)
    kk, vv = with_prev(kb), with_prev(vb)
    sc = jnp.einsum('brnqhd,brnkhd->brnhqk', qb, kk).astype(jnp.float32) * (d ** -0.5)
    qi = jnp.arange(B_BLOCK)[:, None]
    kj = jnp.arange(2 * B_BLOCK)[None, :]
    rel = qi - kj + B_BLOCK
    kabs = jnp.arange(nb)[:, None, None] * B_BLOCK + kj - B_BLOCK
    valid = (rel >= 0) & (rel <= nwin) & (kabs >= 0)
    sc = jnp.where(valid[None, None, :, None], sc, -jnp.inf)
    m = jnp.max(sc, axis=-1, keepdims=True)
    p = jnp.exp(sc - m)
    den = jnp.sum(p, axis=-1, keepdims=True)
    o = jnp.einsum('brnhqk,brnkhd->brnqhd', (p / den).astype(v.dtype), vv)
    lse = (m + jnp.log(den))[..., 0].transpose(0, 1, 2, 4, 3)

    def from_blocks(t):
        t = t.reshape((b, dil, Lp) + t.shape[4:])[:, :, :L]
        t = jnp.moveaxis(t, 1, 2)
        return t.reshape((b, s) + t.shape[3:])

    return from_blocks(o), from_blocks(lse)


def causal_depthwise_conv(u, w, bias):
    s = u.shape[1]
    up = jnp.pad(u, ((0, 0), (CONV_WIDTH - 1, 0), (0, 0)))
    out = sum(up[:, j:j + s] * w[j] for j in range(CONV_WIDTH))
    return out + bias


def setup_inputs(seed: int = 0) -> dict:
    key = jax.random.key(seed)
    ks = jax.random.split(key, 20)
    f32 = jnp.float32

    def nrm(k, shape, scale):
        return jax.random.normal(k, shape, f32) * scale

    def gain(k, shape):
        return 1.0 + 0.05 * jax.random.normal(k, shape, f32)

    x = jax.random.normal(ks[0], (BATCH, SEQ, D_MODEL), f32)
    offset = jax.random.randint(ks[1], (BATCH, 1), 0, 1024, dtype=jnp.int32)
    positions = (offset + jnp.arange(SEQ, dtype=jnp.int32)[None, :]).astype(jnp.int32)
    return {
        "x": x,
        "positions": positions,
        "pre_mix_g": gain(ks[2], (DEPTH, D_MODEL)),
        "w_in": nrm(ks[3], (DEPTH, D_MODEL, IN_COLS), D_MODEL ** -0.5),
        "diff_lambda": nrm(ks[4], (DEPTH, 4, A_QK_DIM), 0.1),
        "diff_head_g": gain(ks[5], (DEPTH, A_V_DIM)),
        "w_a_out": nrm(ks[6], (DEPTH, A_WIDTH, D_MODEL), A_WIDTH ** -0.5),
        "w_b_out": nrm(ks[7], (DEPTH, B_WIDTH, D_MODEL), B_WIDTH ** -0.5),
        "w_mix_out": nrm(ks[8], (DEPTH, D_MODEL, D_MODEL), D_MODEL ** -0.5),
        "post_mix_g": gain(ks[9], (DEPTH, D_MODEL)),
        "pre_ffn_g": gain(ks[10], (DEPTH, D_MODEL)),
        "w_up": nrm(ks[11], (DEPTH, D_MODEL, 2 * D_FF), D_MODEL ** -0.5),
        "conv_w": nrm(ks[12], (DEPTH, CONV_WIDTH, 2 * D_FF), CONV_WIDTH ** -0.5),
        "conv_b": nrm(ks[13], (DEPTH, 2 * D_FF), 0.01),
        "w_down": nrm(ks[14], (DEPTH, D_FF, D_MODEL), D_FF ** -0.5),
        "post_ffn_g": gain(ks[15], (DEPTH, D_MODEL)),
    }


def reference(x, positions, pre_mix_g, w_in, diff_lambda, diff_head_g, w_a_out, w_b_out,
              w_mix_out, post_mix_g, pre_ffn_g, w_up, conv_w, conv_b, w_down, post_ffn_g):
    b, s, _ = x.shape
    cos_a, sin_a = rope_tables(positions, A_QK_DIM)
    cos_b, sin_b = rope_tables(positions, B_HEAD_DIM)
    split_at = np.cumsum([A_Q_COLS, A_K_COLS, A_V_COLS, B_COLS, B_COLS, B_COLS]).tolist()

    for l in range(DEPTH):
        h = rms_norm(x, pre_mix_g[l])
        proj = h @ w_in[l]
        qa, ka, va, qb, kb, vb, gates = jnp.split(proj, split_at, axis=-1)

        qa = apply_rope(qa.reshape(b, s, A_HEADS, 2, A_QK_DIM), cos_a, sin_a)
        ka = apply_rope(ka.reshape(b, s, A_HEADS, 2, A_QK_DIM), cos_a, sin_a)
        va = va.reshape(b, s, A_HEADS, A_V_DIM)
        lam_init = 0.8 - 0.6 * math.exp(-0.3 * l)
        lv = diff_lambda[l].astype(jnp.float32)
        lam = jnp.exp(jnp.sum(lv[0] * lv[1])) - jnp.exp(jnp.sum(lv[2] * lv[3])) + lam_init
        oa = diff_attention(qa, ka, va, lam)
        oa = rms_norm(oa, diff_head_g[l]) * (1.0 - lam_init)
        ya = oa.reshape(b, s, A_WIDTH) @ w_a_out[l]

        qb = apply_rope(qb.reshape(b, s, B_GROUPS, B_HEADS, B_HEAD_DIM), cos_b, sin_b)
        kb = apply_rope(kb.reshape(b, s, B_GROUPS, B_HEADS, B_HEAD_DIM), cos_b, sin_b)
        vb = vb.reshape(b, s, B_GROUPS, B_HEADS, B_HEAD_DIM)
        outs, lses = [], []
        for g, (window, dil) in enumerate(B_PAIRS):
            o_g, lse_g = dilated_group(qb[:, :, g], kb[:, :, g], vb[:, :, g], window, dil)
            outs.append(o_g)
            lses.append(lse_g)
        wts = jax.nn.softmax(jnp.stack(lses, axis=0), axis=0)
        ob = jnp.einsum('gbsh,gbshd->bshd', wts.astype(x.dtype), jnp.stack(outs, axis=0))
        yb = ob.reshape(b, s, B_WIDTH) @ w_b_out[l]

        g_a, g_b = jnp.split(jax.nn.sigmoid(gates), 2, axis=-1)
        mix = (g_a * ya + g_b * yb) @ w_mix_out[l]
        x = x + rms_norm(mix, post_mix_g[l])

        h = rms_norm(x, pre_ffn_g[l])
        u = causal_depthwise_conv(h @ w_up[l], conv_w[l], conv_b[l])
        gate, val = jnp.split(u, 2, axis=-1)
        y = (jax.nn.gelu(gate, approximate=True) * val) @ w_down[l]
        x = x + rms_norm(y, post_ffn_g[l])
    return x
```

```python
import math
from contextlib import ExitStack

import numpy as np
import concourse.bass as bass
import concourse.mybir as mybir
from concourse.bass_utils import run_bass_kernel_spmd

F32 = mybir.dt.float32
BF16 = mybir.dt.bfloat16
I32 = mybir.dt.int32
AF = mybir.ActivationFunctionType
ALU = mybir.AluOpType

S_ = 4096
D = 1024
NT = 32
DEPTH = 4
IN_COLS = 8192
D_FF = 2816
NFF = 22
EPS = 1e-6
B_PAIRS = ((128, 1), (512, 4), (2048, 16))
TWO_PI = 2.0 * math.pi
C1 = float(np.float32(TWO_PI))
C2 = float(TWO_PI - C1)
GEN = 28000
import os as _os
FILL_N = int(_os.environ.get("K_FILL_N", "0"))
EMBED = int(_os.environ.get("K_EMBED", "1"))


class Op:
    __slots__ = ("eng", "fn", "reads", "writes", "waits", "signal", "dma_key", "token", "_deps", "barrier", "post_tags", "_post", "waits_post")

    def __init__(self, eng, fn, reads, writes, dma_key):
        self.eng = eng
        self.fn = fn
        self.reads = reads
        self.writes = writes
        self.dma_key = dma_key
        self.waits = []
        self.signal = False
        self.token = None
        self.barrier = False
        self.post_tags = None
        self._post = set()
        self.waits_post = []


def _same_stream(a, b):
    if a.dma_key is None and b.dma_key is None:
        return a.eng == b.eng
    return a.dma_key is not None and a.dma_key == b.dma_key


class Sched:
    ENGS = ("pe", "act", "dve", "pool", "sp")

    def __init__(self, same_engine_sync=True):
        self.ops = []
        self.same_engine_sync = same_engine_sync

    def add(self, eng, fn, reads=(), writes=(), dma_key=None):
        op = Op(eng, fn, tuple(reads), tuple(writes), dma_key)
        self.ops.append(op)
        return op

    def barrier(self):
        op = Op(None, None, (), (), None)
        op.barrier = True
        self.ops.append(op)

    def analyze(self):
        last_w = {}
        readers = {}
        last_stream = {}
        pending = {e: [] for e in self.ENGS}
        for op in self.ops:
            if op.barrier:
                allp = list(last_stream.values())
                for e in self.ENGS:
                    pending[e] = allp
                last_w = {}
                readers = {}
                continue
            deps = set()
            if op.dma_key is not None:
                prevd = last_stream.get(("dma", op.dma_key))
                if prevd is not None:
                    deps.add(prevd)
            if pending[op.eng]:
                deps.update(pending[op.eng])
                pending[op.eng] = []
            pre = set(deps)
            pt = op.post_tags
            for t in op.reads:
                w = last_w.get(t)
                if w is not None:
                    deps.add(w)
                    if pt is None or t not in pt:
                        pre.add(w)
            for t in op.writes:
                w = last_w.get(t)
                if w is not None:
                    deps.add(w)
                    if pt is None or t not in pt:
                        pre.add(w)
                for r in readers.get(t, ()):
                    if not (r.dma_key is None and op.dma_key is None and r.eng == op.eng
                            and (r.eng == "pe" or not self.same_engine_sync)):
                        deps.add(r)
                        if pt is None or t not in pt:
                            pre.add(r)
            op._deps = deps
            op._post = deps - pre
            for t in op.writes:
                last_w[t] = op
                readers[t] = []
            for t in op.reads:
                lst = readers.setdefault(t, [])
                lst[:] = [r for r in lst if not _same_stream(r, op)]
                lst.append(op)
            last_stream[("dma", op.dma_key) if op.dma_key is not None else ("eng", op.eng)] = op
        ops = [o for o in self.ops if not o.barrier]
        for op in ops:
            keep = []
            for d in op._deps:
                if d is op:
                    continue
                if d.dma_key is None and op.dma_key is None and d.eng == op.eng:
                    if d.eng == "pe" or not self.same_engine_sync:
                        continue
                keep.append(d)
            op._deps = keep
            for d in keep:
                d.signal = True
        cnt = {}
        tot = {}
        for op in ops:
            if op.dma_key is not None:
                base = ("dma", op.dma_key)
                step = 16
            elif op.signal:
                base = ("eng", op.eng)
                step = 1
            else:
                continue
            op.signal = True
            n = tot.get(base, 0)
            gen = (n * step) // GEN
            tot[base] = n + 1
            k = base + (gen,)
            cnt[k] = cnt.get(k, 0) + step
            op.token = (k, cnt[k])
        self.sem_keys = list(cnt.keys())
        seen = {e: {} for e in self.ENGS}
        for op in ops:
            need = {}
            needp = {}
            for d in op._deps:
                k, v = d.token
                if seen[op.eng].get(k, 0) >= v:
                    continue
                tgt = needp if (d in op._post) else need
                if tgt.get(k, 0) < v:
                    tgt[k] = v
            for k, v in list(needp.items()):
                if need.get(k, 0) >= v:
                    del needp[k]
            for k, v in list(need.items()) + list(needp.items()):
                if seen[op.eng].get(k, 0) < v:
                    seen[op.eng][k] = v
            op.waits = list(need.items())
            op.waits_post = list(needp.items())
        self.final_counts = cnt
        self.ops = ops

    def emit(self, nc):
        self.analyze()
        with ExitStack() as es:
            sems = {}
            for i, k in enumerate(self.sem_keys):
                sems[k] = es.enter_context(nc.semaphore("s%d" % i))
            block = es.enter_context(nc.Block())
            by_eng = {e: [op for op in self.ops if op.eng == e] for e in self.ENGS}

            def run(e, ops, last=False):
                for op in ops:
                    for k, v in op.waits:
                        e.wait_ge(sems[k], v)
                    for k, v in op.waits_post[1:]:
                        e.wait_ge(sems[k], v)
                    ins = op.fn(e)
                    if op.waits_post:
                        k, v = op.waits_post[0]
                        ins._wait_ge(sems[k], v)
                    if op.signal:
                        k, v = op.token
                        ins.then_inc(sems[k], 16 if k[0] == "dma" else 1)
                if last:
                    for k, v in self.final_counts.items():
                        if k[0] == "dma":
                            e.wait_ge(sems[k], v)

            @block.tensor
            def _(e):
                run(e, by_eng["pe"])

            @block.scalar
            def _(e):
                run(e, by_eng["act"])

            @block.vector
            def _(e):
                run(e, by_eng["dve"])

            @block.gpsimd
            def _(e):
                run(e, by_eng["pool"])

            @block.sync
            def _(e):
                run(e, by_eng["sp"], last=True)


class SB:
    cnt = 0

    def __init__(self, nc, limit=150016, base=16640):
        self.nc = nc
        self.off = base
        self.limit = limit
        self.n = 0

    def alloc(self, name, shape, dtype):
        esz = 4 if dtype in (F32, I32) else 2
        nbytes = esz
        for s in shape[1:]:
            nbytes *= s
        nbytes = (nbytes + 63) // 64 * 64
        assert self.off + nbytes <= self.limit, ("SBUF overflow", name, self.off, nbytes)
        SB.cnt += 1
        t = self.nc.alloc_sbuf_tensor_at("%s_%d" % (name, SB.cnt), list(shape), dtype, offset=self.off)
        self.off += nbytes
        return t

    def mark(self):
        return self.off

    def reset(self, m):
        self.off = m


class Builder:
    def __init__(self, nc, n_layers, taps=(), stop_after=None):
        self.nc = nc
        self.S = Sched()
        self.sb = SB(nc)
        self.sbt = SB(nc, limit=215552, base=150016)
        self.L = n_layers
        self.taps = set(taps)
        self.stop_after = stop_after
        self.uid = 0
        self.keymap = {}

    def dma(self, eng, out, in_, r, w, key):
        km = self.keymap.setdefault(eng, {})
        key = km.setdefault(key, "%s%d" % (eng, len(km)))
        self.S.add(eng, lambda e: e.dma_start(out=out, in_=in_), r, w, dma_key=key)

    def barrier(self):
        self.S.barrier()
        self.keymap = {}

    def mm(self, out, lhsT, rhs, start, stop, r, w, skip=False, post=None):
        if skip:
            op = self.S.add("pe", lambda e: e.matmul(out, lhsT, rhs, start=start, stop=stop, skip_group_check=True), r, w)
        else:
            op = self.S.add("pe", lambda e: e.matmul(out, lhsT, rhs, start=start, stop=stop), r, w)
        if post is not None and EMBED:
            op.post_tags = set(post)

    def tr(self, out, in_, ident, r, w):
        self.S.add("pe", lambda e: e.transpose(out, in_, ident), r, w)

    def act(self, out, in_, func, r, w, **kw):
        self.S.add("act", lambda e: e.activation(out=out, in_=in_, func=func, **kw), r, w)

    def ts(self, eng, out, in0, s1, s2, op0, op1, r, w):
        if s2 is None:
            self.S.add(eng, lambda e: e.tensor_scalar(out=out, in0=in0, scalar1=s1, scalar2=None, op0=op0), r, w)
        else:
            self.S.add(eng, lambda e: e.tensor_scalar(out=out, in0=in0, scalar1=s1, scalar2=s2, op0=op0, op1=op1), r, w)

    def stt(self, eng, out, in0, scalar, in1, op0, op1, r, w):
        self.S.add(eng, lambda e: e.scalar_tensor_tensor(out=out, in0=in0, scalar=scalar, in1=in1, op0=op0, op1=op1), r, w)

    def tt(self, eng, out, in0, in1, op, r, w):
        self.S.add(eng, lambda e: e.tensor_tensor(out=out, in0=in0, in1=in1, op=op), r, w)

    def cp(self, eng, out, in_, r, w):
        if eng == "act":
            self.S.add("act", lambda e: e.activation(out=out, in_=in_, func=AF.Copy), r, w)
        else:
            self.S.add(eng, lambda e: e.tensor_copy(out=out, in_=in_), r, w)

    def memset(self, eng, out, val, r, w):
        self.S.add(eng, lambda e: e.memset(out, val), r, w)

    def recip(self, out, in_, r, w):
        self.S.add("dve", lambda e: e.reciprocal(out=out, in_=in_), r, w)

    def rstd(self, ss, tmp, out, n, tag):
        self.ts("dve", tmp, ss, 1.0 / n, EPS, ALU.mult, ALU.add, [tag + "ss"], [tag + "ms"])
        self.act(tmp, tmp, AF.Sqrt, [tag + "ms"], [tag + "ms"])
        self.recip(out, tmp, [tag + "ms"], [tag + "rstd"])

    def dram(self, name, shape, dtype):
        kind = "ExternalOutput" if name in self.taps else "Internal"
        return self.nc.dram_tensor(name, list(shape), dtype, kind=kind).ap()

    def declare(self):
        nc = self.nc
        ein = lambda n, s, d=F32: nc.dram_tensor(n, list(s), d, kind="ExternalInput").ap()
        self.x = ein("x", [S_, D])
        self.pos = ein("pos", [1, S_], I32)
        self.pre_mix_g = ein("pre_mix_g", [DEPTH, D])
        self.w_in = ein("w_in", [DEPTH, D, IN_COLS])
        self.diff_lambda = ein("diff_lambda", [DEPTH, 256])
        self.diff_head_g = ein("diff_head_g", [DEPTH, 128])
        self.w_a_out = ein("w_a_out", [DEPTH, 512, D])
        self.w_b_out = ein("w_b_out", [DEPTH, 512, D])
        self.w_mix_out = ein("w_mix_out", [DEPTH, D, D])
        self.post_mix_g = ein("post_mix_g", [DEPTH, D])
        self.pre_ffn_g = ein("pre_ffn_g", [DEPTH, D])
        self.w_up = ein("w_up", [DEPTH, D, 2 * D_FF])
        self.conv_wh = ein("conv_wh", [DEPTH, 128, 2 * NFF * 3])
        self.conv_bh = ein("conv_bh", [DEPTH, 128, 2 * NFF])
        self.w_down = ein("w_down", [DEPTH, D_FF, D])
        self.post_ffn_g = ein("post_ffn_g", [DEPTH, D])
        self.cbf_d = ein("cbf", [128, 1024])
        self.crope_d = ein("crope", [128, 4])
        self.out = nc.dram_tensor("out", [S_, D], F32, kind="ExternalOutput").ap()
        self.xr = self.dram("xr", [S_, D], F32)
        self.ropeA = self.dram("ropeA", [2, 128, S_], F32)
        self.ropeB = self.dram("ropeB", [2, 128, S_], F32)
        self.qaT = self.dram("qaT", [512, S_], BF16)
        self.kaT = self.dram("kaT", [512, S_], BF16)
        self.qbT = self.dram("qbT", [1536, S_], BF16)
        self.kbT = self.dram("kbT", [1536, S_], BF16)
        self.va = self.dram("va", [S_, 512], BF16)
        self.vb = self.dram("vb", [S_, 1536], BF16)
        self.gT = self.dram("gT", [2048, S_], BF16)
        self.oaT = self.dram("oaT", [512, S_], BF16)
        self.obT = self.dram("obT", [512, S_], BF16)
        self.aT = self.dram("aT", [D_FF, S_], BF16)

    def consts(self):
        sb = self.sb
        self.cbf = sb.alloc("cbf", [128, 1024], BF16)
        self.crope = sb.alloc("crope", [128, 4], F32)
        self.dma("pool", self.cbf[:], self.cbf_d, [], ["cbf"], "cbf")
        self.dma("sp", self.crope[:], self.crope_d, [], ["crope"], "crope")
        self.ident = self.cbf[:, 0:128]
        self.permA = self.cbf[:, 128:256]
        self.permB = self.cbf[:, 256:384]
        self.maskU = self.cbf[:, 384:512]
        self.maskUL = self.cbf[:, 384:640]
        self.ones = self.cbf[:, 640:768]
        self.negU = self.cbf[:, 768:896]
        self.negUL = self.cbf[:, 768:1024]
        self.ps = self.nc.alloc_psum_tensor("psall", [128, 4096], F32)
        self.psb = self.ps.bitcast(BF16)
        self.barrier()

    def bank(self, i, n=512, off=0):
        return self.ps[:, i * 512 + off:i * 512 + off + n]

    def bank_bf(self, i, n=1024, off=0):
        return self.psb[:, i * 1024 + off:i * 1024 + off + n]

    def phase0(self):
        sb = self.sb
        m = sb.mark()
        posi = sb.alloc("posi", [128, S_], I32)
        posf = sb.alloc("posf", [128, S_], F32)
        ang = sb.alloc("ang", [128, S_], F32)
        kk = sb.alloc("kk", [128, S_], I32)
        kf = sb.alloc("kf", [128, S_], F32)
        rr = sb.alloc("rr", [128, S_], F32)
        tab = [sb.alloc("tab%d" % i, [128, S_], F32) for i in range(2)]
        self.dma("sp", posi[:], self.pos.broadcast_to([128, S_]), [], ["posi"], "posi")
        self.cp("dve", posf[:], posi[:], ["posi"], ["posf"])
        n = 0
        for ti, dst in enumerate((self.ropeA, self.ropeB)):
            self.ts("dve", ang[:], posf[:], self.crope[:, ti:ti + 1], None, ALU.mult, None, ["posf"], ["ang"])
            for which in range(2):
                shift = math.pi / 2 if which == 0 else 0.0
                tb = tab[n % 2]
                tg = "tab%d" % (n % 2)
                n += 1
                self.ts("dve", rr[:], ang[:], shift, None, ALU.add, None, ["ang"], ["rr"])
                self.ts("dve", kk[:], rr[:], 1.0 / TWO_PI, None, ALU.mult, None, ["rr"], ["kk"])
                self.cp("dve", kf[:], kk[:], ["kk"], ["kf"])
                self.stt("dve", rr[:], kf[:], -C1, rr[:], ALU.mult, ALU.add, ["kf", "rr"], ["rr"])
                self.stt("dve", rr[:], kf[:], -C2, rr[:], ALU.mult, ALU.add, ["kf", "rr"], ["rr"])
                self.ts("dve", rr[:], rr[:], 3.1415925, -3.1415925, ALU.min, ALU.max, ["rr"], ["rr"])
                self.act(tb[:], rr[:], AF.Sin, ["rr"], [tg])
                if which == 1:
                    self.ts("pool", tb[:], tb[:], self.crope[:, 2 + ti:3 + ti], None, ALU.mult, None, [tg], [tg])
                self.dma("sp", dst[which], tb[:], [tg], [], tg)
        self.barrier()
        sb.reset(m)

    def build_hT(self, l, xsrc, gvec, hT, tagp):
        sb = self.sb
        g_b = sb.alloc("g_b", [128, D], F32)
        NS = 4
        xt = [sb.alloc("xt%d" % i, [128, D], F32) for i in range(NS)]
        hb = [sb.alloc("hb%d" % i, [128, D], BF16) for i in range(NS)]
        junk = sb.alloc("junk", [128, D], BF16)
        sm = sb.alloc("sm", [128, 3 * NS], F32)
        self.dma("sp", g_b[:], gvec[l:l + 1, :].broadcast_to([128, D]), [], ["g_b"], tagp + "g_b")

        def front(t):
            s = t % NS
            T = tagp + "%d" % s
            self.dma("sp", xt[s][:], xsrc[t * 128:(t + 1) * 128, :], [], [T + "xt"], T + "xt")
            self.act(junk[:], xt[s][:], AF.Square, [T + "xt"], [T + "ss"], accum_out=sm[:, s:s + 1])
            self.rstd(sm[:, s:s + 1], sm[:, NS + s:NS + s + 1], sm[:, 2 * NS + s:2 * NS + s + 1], D, T)
            self.stt("dve", hb[s][:], xt[s][:], sm[:, 2 * NS + s:2 * NS + s + 1], g_b[:], ALU.mult, ALU.mult,
                     [T + "xt", T + "rstd", "g_b"], [T + "hb"])

        def back(t):
            s = t % NS
            T = tagp + "%d" % s
            pst = self.bank_bf(s)
            for kc in range(8):
                self.tr(pst[:, kc * 128:(kc + 1) * 128], hb[s][:, kc * 128:(kc + 1) * 128], self.ident,
                        [T + "hb"], [T + "pst"])
            self.cp("act" if t % 2 == 0 else "dve", hT[:, :, t * 128:(t + 1) * 128],
                    pst.rearrange("p (k t) -> p k t", k=8), [T + "pst"], [])

        front(0)
        front(1)
        for t in range(NT):
            if t + 2 < NT:
                front(t + 2)
            back(t)

    def p1(self, l, xsrc):
        sb = self.sb
        m = sb.mark()
        hT = sb.alloc("hT", [128, 8, S_], BF16)
        m2 = sb.mark()
        self.sbt.reset(150016)
        rope = [self.sbt.alloc("ropeA", [128, 2, S_], F32), self.sbt.alloc("ropeB", [128, 2, S_], F32)]
        for i, src in enumerate((self.ropeA, self.ropeB)):
            for w in range(2):
                self.dma("sp", rope[i][:, w, :], src[w], [], ["rope%d%d" % (i, w)], "rope%d%d" % (i, w))
        self.build_hT(l, xsrc, self.pre_mix_g, hT, "p1a")
        self.barrier()
        sb.reset(m2)
        if self.stop_after == "p1a":
            dbg = self.dram("hTd", [128, 8 * S_], BF16)
            self.dma("sp", dbg, hT[:].rearrange("p k t -> p (k t)"), [], [], "dbg")
            self.barrier()
            sb.reset(m)
            return
        wblk = [sb.alloc("wblk%d" % i, [128, 8, 512], BF16) for i in range(2)]
        tb = [sb.alloc("tb%d" % i, [128, 512], BF16) for i in range(2)]
        tmp = [sb.alloc("tmp%d" % i, [128, 512], F32) for i in range(2)]
        t2 = [sb.alloc("t2%d" % i, [128, 512], F32) for i in range(2)]
        ot = [sb.alloc("ot%d" % i, [128, 512], BF16) for i in range(4)]
        w_l = self.w_in[l].rearrange("(kc p) c -> p kc c", p=128)
        blocks = []
        blocks.append((0, "ra", self.qaT, 0))
        blocks.append((512, "ra", self.kaT, 0))
        blocks.append((1024, "v", self.va, 0))
        for i in range(3):
            blocks.append((1536 + 512 * i, "rb", self.qbT, 512 * i))
        for i in range(3):
            blocks.append((3072 + 512 * i, "rb", self.kbT, 512 * i))
        for i in range(3):
            blocks.append((4608 + 512 * i, "v", self.vb, 512 * i))
        for i in range(4):
            blocks.append((6144 + 512 * i, "g", self.gT, 512 * i))
        it = 0
        io = 0
        if self.stop_after == "p1w":
            self.dma("pool", wblk[0][:], w_l[:, :, 0:512], [], ["wblk0"], "wblk0")
            dbg = self.dram("wd", [128, 8 * 512], BF16)
            self.dma("sp", dbg, wblk[0][:].rearrange("p k t -> p (k t)"), ["wblk0"], [], "dbg")
            dbg2 = self.dram("rd", [128, 2 * S_], F32)
            self.dma("sp", dbg2, rope[1][:].rearrange("p k t -> p (k t)"), ["rope10", "rope11"], [], "dbg2")
            self.barrier()
            sb.reset(m)
            return
        import os
        dbg_nb = int(os.environ.get("K_DBG_NB", "99"))
        dbg_ns = int(os.environ.get("K_DBG_NS", "4"))
        dbg_sel = os.environ.get("K_DBG_SEL", "")
        if dbg_sel:
            blocks = [blocks[int(c)] for c in dbg_sel.split(",")]
        blocks = blocks[:dbg_nb]

        def wload(bi):
            c0 = blocks[bi][0]
            self.dma("pool", wblk[bi % 2][:], w_l[:, :, c0:c0 + 512], [], ["wblk%d" % (bi % 2)], "wblk%d" % (bi % 2))

        wload(0)
        for bi, (c0, kind, dst, d0) in enumerate(blocks):
            s = bi % 2
            W = "wblk%d" % s
            if bi + 1 < len(blocks):
                wload(bi + 1)
            if kind == "v":
                items = [(0, tt) for tt in range(NT)]
            else:
                items = [(sub, tc) for sub in range(dbg_ns) for tc in range(8)]
            base = it

            def G(n, s=s, W=W, kind=kind, items=items, base=base):
                pi = (base + n) % 2
                ps = self.bank(pi)
                sub, tc = items[n]
                for kc in range(8):
                    if kind == "v":
                        self.mm(ps, hT[:, kc, tc * 128:(tc + 1) * 128], wblk[s][:, kc, :], kc == 0, kc == 7, [W], ["psA%d" % pi])
                    else:
                        self.mm(ps, wblk[s][:, kc, sub * 128:(sub + 1) * 128], hT[:, kc, tc * 512:(tc + 1) * 512],
                                kc == 0, kc == 7, [W], ["psA%d" % pi])

            def post(n, kind=kind, items=items, base=base, dst=dst, d0=d0):
                nonlocal io
                pi = (base + n) % 2
                P = "psA%d" % pi
                ps = self.bank(pi)
                sub, tc = items[n]
                o = io % 4
                io += 1
                O = "ot%d" % o
                if kind == "v":
                    self.cp("dve" if tc % 2 == 0 else "act", ot[o][:], ps, [P], [O])
                    self.dma("sp", dst[tc * 128:(tc + 1) * 128, d0:d0 + 512], ot[o][:], [O], [], O)
                    return
                tsl = slice(tc * 512, (tc + 1) * 512)
                if kind == "g":
                    self.act(ot[o][:], ps, AF.Sigmoid, [P], [O])
                else:
                    ri = 0 if kind == "ra" else 1
                    perm = self.permA if kind == "ra" else self.permB
                    P2 = "psB%d" % pi
                    ps2 = self.bank(2 + pi)
                    self.mm(ps2, perm, tb[pi][:], True, True, ["tb%d" % pi], [P2])
                    self.tt("dve", tmp[pi][:], ps, rope[ri][:, 0, tsl], ALU.mult, [P, "rope%d0" % ri], ["tmp%d" % pi, P])
                    self.tt("dve", t2[pi][:], ps2, rope[ri][:, 1, tsl], ALU.mult, [P2, "rope%d1" % ri], ["t2%d" % pi])
                    self.tt("pool", ot[o][:], tmp[pi][:], t2[pi][:], ALU.add, ["tmp%d" % pi, "t2%d" % pi], [O])
                r0 = d0 + sub * 128
                self.dma("sp", dst[r0:r0 + 128, tsl], ot[o][:], [O], [], O)

            def pre(n, kind=kind, base=base):
                if kind in ("ra", "rb"):
                    pi = (base + n) % 2
                    self.cp("act", tb[pi][:], self.bank(pi), ["psA%d" % pi], ["tb%d" % pi])

            G(0)
            for n in range(len(items)):
                pre(n)
                if n + 1 < len(items):
                    G(n + 1)
                post(n)
            it += len(items)
        self.barrier()
        sb.reset(m)

    def lam_setup(self, l):
        sb = self.sb
        lam_init = 0.8 - 0.6 * math.exp(-0.3 * l)
        lv = sb.alloc("lv", [128, 256], F32)
        pr = sb.alloc("pr", [128, 128], F32)
        sc = sb.alloc("lsc", [128, 8], F32)
        gh = sb.alloc("gh", [128, 128], F32)
        self.dma("sp", lv[:], self.diff_lambda[l:l + 1, :].broadcast_to([128, 256]), [], ["lv"], "lv")
        self.dma("sp", gh[:], self.diff_head_g[l:l + 1, :].broadcast_to([128, 128]), [], ["gh"], "gh")
        self.tt("dve", pr[:, 0:64], lv[:, 0:64], lv[:, 64:128], ALU.mult, ["lv"], ["pr"])
        self.tt("dve", pr[:, 64:128], lv[:, 128:192], lv[:, 192:256], ALU.mult, ["lv", "pr"], ["pr"])
        self.S.add("dve", lambda e: e.reduce_sum(out=sc[:, 0:1], in_=pr[:, 0:64], axis=mybir.AxisListType.X), ["pr"], ["sc"])
        self.S.add("dve", lambda e: e.reduce_sum(out=sc[:, 1:2], in_=pr[:, 64:128], axis=mybir.AxisListType.X), ["pr", "sc"], ["sc"])
        self.act(sc[:, 2:4], sc[:, 0:2], AF.Exp, ["sc"], ["sc2"])
        self.stt("dve", sc[:, 4:5], sc[:, 3:4], -lam_init, sc[:, 2:3], ALU.add, ALU.subtract, ["sc2"], ["neglam"])
        self.ts("dve", gh[:], gh[:], 1.0 - lam_init, None, ALU.mult, None, ["gh"], ["gh"])
        self.neglam = sc[:, 4:5]
        self.gh = gh

    def phA(self, l):
        sb = self.sb
        m = sb.mark()
        self.lam_setup(l)
        lam_init = 0.8 - 0.6 * math.exp(-0.3 * l)
        qz = [[sb.alloc("qz%d%d" % (i, c), [128, S_], BF16) for c in range(2)] for i in range(2)]
        kT = [sb.alloc("kT%d" % i, [128, S_], BF16) for i in range(2)]
        V = [sb.alloc("V%d" % i, [128, NT, 128], BF16) for i in range(2)]
        for i in range(2):
            self.memset("pool", qz[i][0][64:128, :], 0.0, [], ["qz%d0" % i])
            self.memset("pool", qz[i][1][0:64, :], 0.0, [], ["qz%d1" % i])
        NPT = 6
        pT = [sb.alloc("pT%d" % i, [128, 512], BF16) for i in range(NPT)]
        osb = [[sb.alloc("osb%d%d" % (f, c), [128, 512], F32) for c in range(2)] for f in range(2)]
        Dn = [[sb.alloc("Dn%d%d" % (f, c), [128, 512], F32) for c in range(2)] for f in range(2)]
        o32 = [sb.alloc("o32%d" % i, [128, 512], F32) for i in range(2)]
        o2 = [sb.alloc("o2%d" % i, [128, 512], F32) for i in range(2)]
        sq = [sb.alloc("sq%d" % i, [128, 512], F32) for i in range(2)]
        s2 = [sb.alloc("s2%d" % i, [128, 512], F32) for i in range(2)]
        rs = [sb.alloc("rs%d" % i, [128, 512], F32) for i in range(2)]
        oT = [sb.alloc("oT%d" % i, [128, 512], BF16) for i in range(2)]
        ones32 = sb.alloc("ones32", [128, 128], F32)
        ghc = sb.alloc("ghc", [128, 1], F32)
        self.memset("pool", ones32[:], 1.0, [], ["ones32"])
        self.dma("sp", ghc[:], self.diff_head_g[l:l + 1, :].rearrange("o d -> d o"), [], ["ghc"], "ghc")
        self.ts("dve", ghc[:], ghc[:], 1.0 - lam_init, None, ALU.mult, None, ["ghc"], ["ghc"])

        def load(h):
            s = h % 2
            for c in range(2):
                self.dma("sp", qz[s][c][c * 64:(c + 1) * 64, :], self.qaT[h * 128 + c * 64:h * 128 + (c + 1) * 64, :],
                         ["qz%d%d" % (s, c)], ["qz%d%d" % (s, c)], "qz%d%d" % (s, c))
            self.dma("sp", kT[s][:], self.kaT[h * 128:(h + 1) * 128, :], [], ["kT%d" % s], "kT%d" % s)
            self.dma("sp", V[s][:], self.va[:, h * 128:(h + 1) * 128].rearrange("(t p) d -> p t d", p=128),
                     [], ["V%d" % s], "V%d" % s)

        load(0)
        ip = 0
        pending = []
        for h in range(4):
            s = h % 2
            if h + 1 < 4:
                load(h + 1)
            items = []
            for j in range(8):
                for c in range(2):
                    nkt = 4 * j + 4
                    for kt in range(nkt):
                        q0 = max(j * 512, kt * 128)
                        items.append((j, c, kt, q0, (j + 1) * 512 - q0, nkt))
            base = ip

            def qk(n):
                j, c, kt, q0, N, nkt = items[n]
                i = base + n
                P = "psS%d" % (i % 3)
                diag = kt >= 4 * j
                self.mm(self.bank(i % 3, N), kT[s][:, kt * 128:(kt + 1) * 128],
                        qz[s][c][:, q0:q0 + N], True, not diag, ["kT%d" % s, "qz%d%d" % (s, c)], [P], post=[P])
                if diag:
                    self.mm(self.bank(i % 3, 128), self.ident, self.negU, False, True, [], [P])

            def ex(n):
                j, c, kt, q0, N, nkt = items[n]
                i = base + n
                self.act(pT[i % NPT][:, 0:N], self.bank(i % 3, N), AF.Exp, ["psS%d" % (i % 3)], ["pT%d" % (i % NPT)], scale=0.125)

            def pv(n):
                j, c, kt, q0, N, nkt = items[n]
                i = base + n
                T = "pT%d" % (i % NPT)
                off = q0 - j * 512
                PO = "psO%d" % c
                PDN = "psDn%d" % c
                self.mm(self.bank(3 + c)[:, off:off + N], V[s][:, kt, :], pT[i % NPT][:, 0:N], kt == 0, kt == nkt - 1,
                        [T, "V%d" % s], [PO], post=[T, PO])
                self.mm(self.bank(5 + c)[:, off:off + N], self.ones, pT[i % NPT][:, 0:N], kt == 0, kt == nkt - 1,
                        [T], [PDN], post=[T, PDN])

            def tail(n):
                nonlocal pending
                j, c, kt, q0, N, nkt = items[n]
                f = (h * 8 + j) % 2
                Tg = "fA%d" % f
                if c == 1 and kt == min(7, nkt - 1) and pending:
                    for fn in pending:
                        fn()
                    pending = []
                if kt != nkt - 1:
                    return
                self.cp("dve", osb[f][c][:], self.bank(3 + c), ["psO%d" % c], [Tg + "os%d" % c, "psO%d" % c])
                self.cp("dve", Dn[f][c][:], self.bank(5 + c), ["psDn%d" % c], [Tg + "D%d" % c, "psDn%d" % c])
                if c == 0:
                    return
                self.tt("dve", o32[f][:], osb[f][0][:], Dn[f][1][:], ALU.mult, [Tg + "os0", Tg + "D1"], [Tg + "o"])
                self.tt("pool", o2[f][:], osb[f][1][:], Dn[f][0][:], ALU.mult, [Tg + "os1", Tg + "D0"], [Tg + "o2"])
                self.stt("dve", o32[f][:], o2[f][:], self.neglam, o32[f][:], ALU.mult, ALU.add, [Tg + "o", Tg + "o2", "neglam"], [Tg + "o"])
                self.tt("pool", sq[f][:], o32[f][:], o32[f][:], ALU.mult, [Tg + "o"], [Tg + "sq"])
                self.tt("pool", s2[f][:], Dn[f][0][:], Dn[f][1][:], ALU.mult, [Tg + "D0", Tg + "D1"], [Tg + "s2"])
                self.tt("pool", s2[f][:], s2[f][:], s2[f][:], ALU.mult, [Tg + "s2"], [Tg + "s2"])

                def stageB(f=f, Tg=Tg, h=h, j=j):
                    self.mm(self.bank(7), ones32[:], sq[f][:], True, True, [Tg + "sq", "ones32"], ["psQ"])
                    self.stt("dve", rs[f][:], s2[f][:], 128.0 * EPS, self.bank(7), ALU.mult, ALU.add, ["psQ", Tg + "s2"], [Tg + "rs", "psQ"])
                    self.act(rs[f][:], rs[f][:], AF.Ln, [Tg + "rs"], [Tg + "rs"], scale=1.0 / 128)
                    self.act(rs[f][:], rs[f][:], AF.Exp, [Tg + "rs"], [Tg + "rs"], scale=-0.5)
                    self.stt("dve", oT[f][:], o32[f][:], ghc[:], rs[f][:], ALU.mult, ALU.mult, [Tg + "o", "ghc", Tg + "rs"], ["oT%d" % f])
                    self.dma("sp", self.oaT[h * 128:(h + 1) * 128, j * 512:(j + 1) * 512], oT[f][:], ["oT%d" % f], [], "oT%d" % f)

                pending.append(stageB)

            qk(0)
            qk(1)
            for n in range(len(items)):
                if n + 2 < len(items):
                    qk(n + 2)
                ex(n)
                pv(n)
                tail(n)
            ip += len(items)
        for fn in pending:
            fn()
        self.barrier()
        sb.reset(m)

    def phB(self, l):
        sb = self.sb
        m = sb.mark()
        self.sbt.reset(150016)
        self.Mw = (self.sbt.alloc("wa", [128, 4, D], BF16), self.sbt.alloc("wb", [128, 4, D], BF16),
                   self.sbt.alloc("wm", [128, 8, D], BF16))
        self.dma("pool", self.Mw[0][:], self.w_a_out[l].rearrange("(kc p) c -> p kc c", p=128), [], ["wa"], "wa")
        self.dma("pool", self.Mw[1][:], self.w_b_out[l].rearrange("(kc p) c -> p kc c", p=128), [], ["wb"], "wb")
        self.dma("pool", self.Mw[2][:], self.w_mix_out[l].rearrange("(kc p) c -> p kc c", p=128), [], ["wm"], "wm")
        qT = [sb.alloc("bqT%d" % i, [128, S_], BF16) for i in range(2)]
        kT = [sb.alloc("bkT%d" % i, [128, S_], BF16) for i in range(2)]
        V = [sb.alloc("bV%d" % i, [128, NT, 128], BF16) for i in range(2)]
        accN = sb.alloc("accN", [128, S_], F32)
        accD = sb.alloc("accD", [128, S_], F32)
        pT = [sb.alloc("bpT%d" % i, [128, 256], BF16) for i in range(6)]
        obt = sb.alloc("obt", [128, S_], BF16)
        scale = 128 ** -0.5
        combos = [(h, g) for h in range(4) for g in range(3)]

        def load(ci):
            h, g = combos[ci]
            s = ci % 2
            dil = B_PAIRS[g][1]
            nb = NT // dil
            row = (g * 4 + h) * 128
            self.dma("sp", qT[s][:], self.qbT[row:row + 128, :], [], ["qT%d" % s], "bqT%d" % s)
            self.dma("sp", kT[s][:], self.kbT[row:row + 128, :], [], ["kT%d" % s], "bkT%d" % s)
            vsrc = self.vb[:, row:row + 128]
            for r in range(dil):
                src = vsrc[r:S_:dil, :].rearrange("(b i) d -> i b d", i=128)
                wr = ["V%dr%d" % (s, r)] + (["V%dall" % s] if r == 0 else [])
                self.dma("sp", V[s][:, r * nb:(r + 1) * nb, :], src, [], wr, "bV%d_%d" % (s, r % 4))

        load(0)
        ip = 0
        ig = 0
        for ci, (h, g) in enumerate(combos):
            s = ci % 2
            dil = B_PAIRS[g][1]
            nb = NT // dil
            if ci + 1 < len(combos):
                load(ci + 1)
            items = [(r, b) for r in range(dil) for b in range(nb)]
            base = ip

            def qk(n):
                r, b = items[n]
                i = base + n
                st = r + dil * 128 * b
                nq = 256 if b + 1 < nb else 128
                kcols = slice(st, st + dil * 127 + 1, dil)
                qcols = slice(st, st + dil * (nq - 1) + 1, dil)
                P = "psS%d" % (i % 3)
                self.mm(self.bank(i % 3, nq), kT[s][:, kcols], qT[s][:, qcols], True, False,
                        ["kT%d" % s, "qT%d" % s], [P])
                self.mm(self.bank(i % 3, nq), self.ident, self.negUL[:, 0:nq], False, True, [], [P])

            def ex(n):
                r, b = items[n]
                i = base + n
                nq = 256 if b + 1 < nb else 128
                self.act(pT[i % 6][:, 0:nq], self.bank(i % 3, nq), AF.Exp, ["psS%d" % (i % 3)], ["pT%d" % (i % 6)], scale=scale)

            state = {"gi": None}

            def pv(n):
                nonlocal ig
                r, b = items[n]
                i = base + n
                T = "pT%d" % (i % 6)
                if b % 4 == 0:
                    state["gi"] = ig % 2
                    ig += 1
                gi = state["gi"]
                psN = self.bank(3 + gi)
                psD = self.bank(5 + gi)
                PN = "psN%d" % gi
                PD = "psD%d" % gi
                n_cur = r * nb + b
                col = (b % 4) * 128
                VT = ["V%dr%d" % (s, r), "V%dall" % s]
                for (pso, PT_, lw) in ((psN, PN, None), (psD, PD, self.ones)):
                    first = True
                    if b > 0:
                        ipv = i - 1
                        lhs = V[s][:, n_cur - 1, :] if lw is None else lw
                        self.mm(pso[:, col:col + 128], lhs, pT[ipv % 6][:, 128:256], True, False,
                                ["pT%d" % (ipv % 6)] + VT, [PT_])
                        first = False
                    lhs = V[s][:, n_cur, :] if lw is None else lw
                    self.mm(pso[:, col:col + 128], lhs, pT[i % 6][:, 0:128], first, True, [T] + VT, [PT_])
                if b % 4 == 3 or b == nb - 1:
                    b0 = (b // 4) * 4
                    nblk = b - b0 + 1
                    st0 = r + dil * 128 * b0
                    tsl = slice(st0, st0 + dil * (128 * nblk - 1) + 1, dil)
                    nn = nblk * 128
                    if g == 0:
                        self.cp("dve", accN[:, tsl], psN[:, 0:nn], [PN], ["accN", PN])
                        self.cp("dve", accD[:, tsl], psD[:, 0:nn], [PD], ["accD", PD])
                    else:
                        self.tt("dve", accN[:, tsl], accN[:, tsl], psN[:, 0:nn], ALU.add, [PN, "accN"], ["accN", PN])
                        self.tt("dve", accD[:, tsl], accD[:, tsl], psD[:, 0:nn], ALU.add, [PD, "accD"], ["accD", PD])

            qk(0)
            if len(items) > 1:
                qk(1)
            for n in range(len(items)):
                if n + 2 < len(items):
                    qk(n + 2)
                ex(n)
                pv(n)
            ip += len(items)
            if g == 2:
                for q in range(4):
                    qs = slice(q * 1024, (q + 1) * 1024)
                    self.act(accD[:, qs], accD[:, qs], AF.Ln, ["accD"], ["accD"])
                    self.act(accD[:, qs], accD[:, qs], AF.Exp, ["accD"], ["accD"], scale=-1.0)
                    self.tt("pool", obt[:, qs], accN[:, qs], accD[:, qs], ALU.mult, ["accN", "accD"], ["obt"])
                self.dma("sp", self.obT[h * 128:(h + 1) * 128, :], obt[:], ["obt"], [], "obt")
        self.barrier()
        sb.reset(m)

    def post_norm_res(self, psv, g_b, xtile, dst, k, sm, tmp, xo, tagp, r_extra, w_x):
        f = k % 2
        T = tagp + "%d" % f
        junk = self._pjunk
        self.act(junk[:, 0:512], psv[:, 0:512], AF.Square, r_extra, [T + "ssa"], accum_out=sm[f][:, 0:1])
        self.act(junk[:, 512:1024], psv[:, 512:1024], AF.Square, r_extra, [T + "ssb"], accum_out=sm[f][:, 1:2])
        self.tt("dve", sm[f][:, 2:3], sm[f][:, 0:1], sm[f][:, 1:2], ALU.add, [T + "ssa", T + "ssb"], [T + "ss"])
        self.rstd(sm[f][:, 2:3], sm[f][:, 3:4], sm[f][:, 4:5], D, T)
        self.stt("dve", tmp[f][:], psv, sm[f][:, 4:5], g_b[:], ALU.mult, ALU.mult, r_extra + [T + "rstd", tagp + "g_b"], [T + "tmp"])
        self.tt("pool", xo[f][:], tmp[f][:], xtile, ALU.add, [T + "tmp"] + w_x, [T + "xo"])
        self.dma("sp", dst, xo[f][:], [T + "xo"], [], T + "xo")

    def phM(self, l, xsrc, xdst):
        sb = self.sb
        m = sb.mark()
        wa, wb, wm = self.Mw
        g_b = sb.alloc("g_bM", [128, D], F32)
        self.dma("sp", g_b[:], self.post_mix_g[l:l + 1, :].broadcast_to([128, D]), [], ["Mg_b"], "Mg_b")
        oac = [sb.alloc("oac%d" % i, [128, 4, 512], BF16) for i in range(2)]
        obc = [sb.alloc("obc%d" % i, [128, 4, 512], BF16) for i in range(2)]
        gc = [sb.alloc("gc%d" % i, [128, 16, 512], BF16) for i in range(2)]
        xc = [sb.alloc("xc%d" % i, [128, 4, D], F32) for i in range(2)]
        mT = sb.alloc("mT", [128, 8, 512], BF16)
        ta = [sb.alloc("ta%d" % i, [128, 512], F32) for i in range(2)]
        tb_ = [sb.alloc("tbm%d" % i, [128, 512], F32) for i in range(2)]
        sm = [sb.alloc("smM%d" % i, [128, 8], F32) for i in range(2)]
        tmp = [sb.alloc("tmpM%d" % i, [128, D], F32) for i in range(2)]
        xo = [sb.alloc("xoM%d" % i, [128, D], F32) for i in range(2)]
        self._pjunk = sb.alloc("pjunk", [128, D], BF16)

        def load(j):
            s = j % 2
            tsl = slice(j * 512, (j + 1) * 512)
            self.dma("sp", oac[s][:], self.oaT[:, tsl].rearrange("(kc p) t -> p kc t", p=128), [], ["oac%d" % s], "oac%d" % s)
            self.dma("sp", obc[s][:], self.obT[:, tsl].rearrange("(kc p) t -> p kc t", p=128), [], ["obc%d" % s], "obc%d" % s)
            self.dma("sp", gc[s][:], self.gT[:, tsl].rearrange("(kc p) t -> p kc t", p=128), [], ["gc%d" % s], "gc%d" % s)
            self.dma("sp", xc[s][:], xsrc[tsl, :].rearrange("(t p) d -> p t d", p=128), [], ["xc%d" % s], "xc%d" % s)

        load(0)
        k = 0
        iy = 0
        for j in range(8):
            s = j % 2
            if j + 1 < 8:
                load(j + 1)
            for fb in range(8):
                pi = iy % 2
                iy += 1
                psa = self.bank(pi)
                psb = self.bank(2 + pi)
                fsl = slice(fb * 128, (fb + 1) * 128)
                for kc in range(4):
                    self.mm(psa, wa[:, kc, fsl], oac[s][:, kc, :], kc == 0, kc == 3, ["wa", "oac%d" % s], ["psa%d" % pi])
                for kc in range(4):
                    self.mm(psb, wb[:, kc, fsl], obc[s][:, kc, :], kc == 0, kc == 3, ["wb", "obc%d" % s], ["psb%d" % pi])
                self.tt("dve", ta[pi][:], psa, gc[s][:, fb, :], ALU.mult, ["psa%d" % pi, "gc%d" % s], ["ta%d" % pi])
                self.tt("dve", tb_[pi][:], psb, gc[s][:, 8 + fb, :], ALU.mult, ["psb%d" % pi, "gc%d" % s], ["tb%d" % pi])
                self.tt("pool", mT[:, fb, :], ta[pi][:], tb_[pi][:], ALU.add, ["ta%d" % pi, "tb%d" % pi], ["mT%d" % fb])
            for tt in range(4):
                pm = 4 + 2 * (k % 2)
                PM = "psM%d" % (k % 2)
                psv = self.ps[:, pm * 512:(pm + 2) * 512]
                for half in range(2):
                    for kc in range(8):
                        self.mm(psv[:, half * 512:(half + 1) * 512], mT[:, kc, tt * 128:(tt + 1) * 128],
                                wm[:, kc, half * 512:(half + 1) * 512], kc == 0, kc == 7, ["mT%d" % kc, "wm"], [PM])
                row = j * 512 + tt * 128
                self.post_norm_res(psv, g_b, xc[s][:, tt, :], xdst[row:row + 128, :], k, sm, tmp, xo, "M",
                                   [PM], ["xc%d" % s])
                k += 1
        self.barrier()
        sb.reset(m)

    def phF1(self, l, xsrc):
        sb = self.sb
        m = sb.mark()
        hT = sb.alloc("h2T", [128, 8, S_], BF16)
        m2 = sb.mark()
        self.build_hT(l, xsrc, self.pre_ffn_g, hT, "f0")
        self.barrier()
        sb.reset(m2)
        self.sbt.reset(150016)
        self.wd = self.sbt.alloc("wd", [128, NFF, D], BF16)
        wsrc = self.w_down[l].rearrange("(kc p) c -> p kc c", p=128)
        self.dma("pool", self.wd[:, 0:11, :], wsrc[:, 0:11, :], [], ["wd0"], "wd0")
        self.dma("pool", self.wd[:, 11:22, :], wsrc[:, 11:22, :], [], ["wd1"], "wd1")
        cw = sb.alloc("cw", [128, 2 * NFF * 3], F32)
        cb = sb.alloc("cb", [128, 2 * NFF], F32)
        self.dma("sp", cw[:], self.conv_wh[l], [], ["cw"], "cw")
        self.dma("sp", cb[:], self.conv_bh[l], [], ["cw"], "cb")
        wg = [sb.alloc("wg%d" % i, [128, 8, 128], BF16) for i in range(2)]
        wv = [sb.alloc("wv%d" % i, [128, 8, 128], BF16) for i in range(2)]
        ug = [sb.alloc("ug%d" % i, [128, 514], F32) for i in range(2)]
        uv = [sb.alloc("uv%d" % i, [128, 514], F32) for i in range(2)]
        cg = [sb.alloc("cg%d" % i, [128, 512], F32) for i in range(3)]
        cv = [sb.alloc("cv%d" % i, [128, 512], F32) for i in range(3)]
        gg = [sb.alloc("gg%d" % i, [128, 512], F32) for i in range(2)]
        at = [sb.alloc("at%d" % i, [128, 512], BF16) for i in range(4)]
        w_l = self.w_up[l].rearrange("(kc p) c -> p kc c", p=128)

        def wload(i):
            s = i % 2
            self.dma("pool", wg[s][:], w_l[:, :, i * 128:(i + 1) * 128], [], ["wg%d" % s], "wg%d" % s)
            self.dma("pool", wv[s][:], w_l[:, :, D_FF + i * 128:D_FF + (i + 1) * 128], [], ["wv%d" % s], "wv%d" % s)

        items = [(i, tc) for i in range(NFF) for tc in range(8)]

        def front(it):
            i, tc = items[it]
            s = i % 2
            if tc == 0 and i + 1 < NFF:
                wload(i + 1)
            u = it % 2
            up = (it - 1) % 2
            c3 = it % 3
            tsl = slice(tc * 512, (tc + 1) * 512)
            psg = self.bank(u)
            psv = self.bank(2 + u)
            for kc in range(8):
                self.mm(psg, wg[s][:, kc, :], hT[:, kc, tsl], kc == 0, kc == 7, ["wg%d" % s], ["psg%d" % u])
            for kc in range(8):
                self.mm(psv, wv[s][:, kc, :], hT[:, kc, tsl], kc == 0, kc == 7, ["wv%d" % s], ["psv%d" % u])
            for (nm, ub, psx, cc, blk) in (("g", ug, psg, cg, i), ("v", uv, psv, cv, NFF + i)):
                U = "u%s%d" % (nm, u)
                Up = "u%s%d" % (nm, up)
                PS = "ps%s%d" % (nm, u)
                C = "c%s%d" % (nm, c3)
                w0 = cw[:, blk * 3 + 0:blk * 3 + 1]
                w1 = cw[:, blk * 3 + 1:blk * 3 + 2]
                w2 = cw[:, blk * 3 + 2:blk * 3 + 3]
                bb = cb[:, blk:blk + 1]
                self.cp("act", ub[u][:, 2:514], psx, [PS], [U])
                self.act(cc[c3][:], psx, AF.Identity, [PS, "cw"], [C], scale=w2, bias=bb)
                if tc == 0:
                    self.memset("pool", ub[u][:, 0:2], 0.0, [U], [U])
                else:
                    self.cp("pool", ub[u][:, 0:2], ub[up][:, 512:514], [Up, U], [U])
                self.stt("dve", cc[c3][:], ub[u][:, 1:513], w1, cc[c3][:], ALU.mult, ALU.add, [U, "cw", C], [C])
                self.stt("dve", cc[c3][:], ub[u][:, 0:512], w0, cc[c3][:], ALU.mult, ALU.add, [U, "cw", C], [C])

        def back(it):
            i, tc = items[it]
            u = it % 2
            c3 = it % 3
            tsl = slice(tc * 512, (tc + 1) * 512)
            self.act(gg[u][:], cg[c3][:], AF.Gelu_apprx_tanh, ["cg%d" % c3], ["gg%d" % u])
            o = it % 4
            self.tt("pool", at[o][:], gg[u][:], cv[c3][:], ALU.mult, ["gg%d" % u, "cv%d" % c3], ["at%d" % o])
            self.dma("sp", self.aT[i * 128:(i + 1) * 128, tsl], at[o][:], ["at%d" % o], [], "at%d" % o)

        wload(0)
        front(0)
        for it in range(len(items)):
            if it + 1 < len(items):
                front(it + 1)
            back(it)
        self.barrier()
        sb.reset(m)

    def phF2(self, l, xsrc, xdst):
        sb = self.sb
        m = sb.mark()
        wd = self.wd
        g_b = sb.alloc("g_bF", [128, D], F32)
        self.dma("sp", g_b[:], self.post_ffn_g[l:l + 1, :].broadcast_to([128, D]), [], ["Fg_b"], "Fg_b")
        ac = [sb.alloc("ac%d" % i, [128, NFF, 512], BF16) for i in range(2)]
        xc = [sb.alloc("xcF%d" % i, [128, 4, D], F32) for i in range(2)]
        sm = [sb.alloc("smF%d" % i, [128, 8], F32) for i in range(2)]
        tmp = [sb.alloc("tmpF%d" % i, [128, D], F32) for i in range(2)]
        xo = [sb.alloc("xoF%d" % i, [128, D], F32) for i in range(2)]
        self._pjunk = sb.alloc("pjunkF", [128, D], BF16)

        def load(j):
            s = j % 2
            tsl = slice(j * 512, (j + 1) * 512)
            self.dma("sp", ac[s][:], self.aT[:, tsl].rearrange("(kc p) t -> p kc t", p=128), [], ["ac%d" % s], "ac%d" % s)
            self.dma("sp", xc[s][:], xsrc[tsl, :].rearrange("(t p) d -> p t d", p=128), [], ["xc%d" % s], "xcF%d" % s)

        load(0)
        k = 0
        for j in range(8):
            s = j % 2
            if j + 1 < 8:
                load(j + 1)
            for tt in range(4):
                pm = 2 * (k % 4)
                PM = "psF%d" % (k % 4)
                psv = self.ps[:, pm * 512:(pm + 2) * 512]
                for half in range(2):
                    for kc in range(NFF):
                        self.mm(psv[:, half * 512:(half + 1) * 512], ac[s][:, kc, tt * 128:(tt + 1) * 128],
                                wd[:, kc, half * 512:(half + 1) * 512], kc == 0, kc == NFF - 1, ["ac%d" % s, "wd"], [PM])
                row = j * 512 + tt * 128
                self.post_norm_res(psv, g_b, xc[s][:, tt, :], xdst[row:row + 128, :], k, sm, tmp, xo, "F",
                                   [PM], ["xc%d" % s])
                k += 1
        self.barrier()
        sb.reset(m)

    def build(self):
        self.declare()
        self.consts()
        self.phase0()
        stop = self.stop_after
        for l in range(self.L if stop != "p0" else 0):
            xsrc = self.x if l == 0 else self.xr
            last = (l == self.L - 1)
            self.p1(l, xsrc)
            if stop in ("p1", "p1a", "p1w"):
                break
            self.phA(l)
            if stop == "A":
                break
            self.phB(l)
            if stop == "B":
                break
            self.phM(l, xsrc, self.xr)
            if stop == "M":
                break
            self.phF1(l, self.xr)
            if stop == "F1":
                break
            self.phF2(l, self.xr, self.out if last else self.xr)
        self.S.emit(self.nc)


def host_consts():
    k = np.arange(128)
    ident = np.eye(128, dtype=np.float32)
    pa = np.where((k % 64) < 32, k + 32, k - 32)
    pb = np.where(k < 64, k + 64, k - 64)
    permA = np.zeros((128, 128), np.float32)
    permA[pa, k] = 1.0
    permB = np.zeros((128, 128), np.float32)
    permB[pb, k] = 1.0
    maskU = (k[:, None] <= k[None, :]).astype(np.float32)
    maskL = (k[:, None] >= k[None, :]).astype(np.float32)
    ones = np.ones((128, 128), np.float32)
    NEG = np.float32(-30000.0)
    cbf = np.concatenate([ident, permA, permB, maskU, maskL, ones, (1 - maskU) * NEG, (1 - maskL) * NEG], axis=1)
    invA = (np.float32(10000.0) ** (-(np.arange(0, 64, 2, dtype=np.float32)) / np.float32(64))).astype(np.float32)
    invB = (np.float32(10000.0) ** (-(np.arange(0, 128, 2, dtype=np.float32)) / np.float32(128))).astype(np.float32)
    crope = np.zeros((128, 4), np.float32)
    crope[:, 0] = invA[k % 32]
    crope[:, 1] = invB[k % 64]
    crope[:, 2] = np.where((k % 64) < 32, -1.0, 1.0)
    crope[:, 3] = np.where(k < 64, -1.0, 1.0)
    return np.ascontiguousarray(cbf), crope


def make_in_maps(inputs, cores):
    cbf, crope = host_consts()
    f = lambda a: np.ascontiguousarray(np.asarray(a))
    cw = np.asarray(inputs["conv_w"])
    Ld = cw.shape[0]
    cwh = f(cw.transpose(0, 2, 1).reshape(Ld, 2 * NFF, 128, 3).transpose(0, 2, 1, 3).reshape(Ld, 128, 2 * NFF * 3))
    cbh = f(np.asarray(inputs["conv_b"]).reshape(Ld, 2 * NFF, 128).transpose(0, 2, 1))
    shared = {
        "pre_mix_g": f(inputs["pre_mix_g"]), "w_in": f(inputs["w_in"]),
        "diff_lambda": f(np.asarray(inputs["diff_lambda"]).reshape(Ld, 256)),
        "diff_head_g": f(inputs["diff_head_g"]), "w_a_out": f(inputs["w_a_out"]), "w_b_out": f(inputs["w_b_out"]),
        "w_mix_out": f(inputs["w_mix_out"]), "post_mix_g": f(inputs["post_mix_g"]), "pre_ffn_g": f(inputs["pre_ffn_g"]),
        "w_up": f(inputs["w_up"]), "conv_wh": cwh, "conv_bh": cbh, "w_down": f(inputs["w_down"]),
        "post_ffn_g": f(inputs["post_ffn_g"]), "cbf": cbf, "crope": crope,
    }
    x = np.asarray(inputs["x"])
    pos = np.asarray(inputs["positions"]).astype(np.int32)
    maps = []
    for b in cores:
        mp = dict(shared)
        mp["x"] = f(x[b])
        mp["pos"] = f(pos[b][None, :])
        maps.append(mp)
    return maps


def build_program(n_layers=DEPTH, taps=(), stop_after=None):
    nc = bass.Bass("TRN2", target_bir_lowering=False)
    b = Builder(nc, n_layers, taps, stop_after)
    b.build()
    return nc


def kernel(**inputs):
    nc = build_program()
    maps = make_in_maps(inputs, list(range(8)))
    res = run_bass_kernel_spmd(nc, maps, core_ids=list(range(8)))
    return np.stack([np.asarray(r["out"]) for r in res.results], axis=0).astype(np.float32)
```

```python
import math
from contextlib import ExitStack

import numpy as np
import concourse.bass as bass
import concourse.mybir as mybir
from concourse.bass_utils import run_bass_kernel_spmd

F32 = mybir.dt.float32
BF16 = mybir.dt.bfloat16
I32 = mybir.dt.int32
AF = mybir.ActivationFunctionType
ALU = mybir.AluOpType

S_ = 4096
D = 1024
NT = 32
DEPTH = 4
IN_COLS = 8192
D_FF = 2816
NFF = 22
EPS = 1e-6
B_PAIRS = ((128, 1), (512, 4), (2048, 16))
TWO_PI = 2.0 * math.pi
C1 = float(np.float32(TWO_PI))
C2 = float(TWO_PI - C1)
GEN = 28000
import os as _os
FILL_N = int(_os.environ.get("K_FILL_N", "0"))
EMBED = int(_os.environ.get("K_EMBED", "1"))


class Op:
    __slots__ = ("eng", "fn", "reads", "writes", "waits", "signal", "dma_key", "token", "_deps", "barrier", "post_tags", "_post", "waits_post")

    def __init__(self, eng, fn, reads, writes, dma_key):
        self.eng = eng
        self.fn = fn
        self.reads = reads
        self.writes = writes
        self.dma_key = dma_key
        self.waits = []
        self.signal = False
        self.token = None
        self.barrier = False
        self.post_tags = None
        self._post = set()
        self.waits_post = []


def _same_stream(a, b):
    if a.dma_key is None and b.dma_key is None:
        return a.eng == b.eng
    return a.dma_key is not None and a.dma_key == b.dma_key


class Sched:
    ENGS = ("pe", "act", "dve", "pool", "sp")

    def __init__(self, same_engine_sync=True):
        self.ops = []
        self.same_engine_sync = same_engine_sync

    def add(self, eng, fn, reads=(), writes=(), dma_key=None):
        op = Op(eng, fn, tuple(reads), tuple(writes), dma_key)
        self.ops.append(op)
        return op

    def barrier(self):
        op = Op(None, None, (), (), None)
        op.barrier = True
        self.ops.append(op)

    def analyze(self):
        last_w = {}
        readers = {}
        last_stream = {}
        pending = {e: [] for e in self.ENGS}
        for op in self.ops:
            if op.barrier:
                allp = list(last_stream.values())
                for e in self.ENGS:
                    pending[e] = allp
                last_w = {}
                readers = {}
                continue
            deps = set()
            if op.dma_key is not None:
                prevd = last_stream.get(("dma", op.dma_key))
                if prevd is not None:
                    deps.add(prevd)
            if pending[op.eng]:
                deps.update(pending[op.eng])
                pending[op.eng] = []
            pre = set(deps)
            pt = op.post_tags
            for t in op.reads:
                w = last_w.get(t)
                if w is not None:
                    deps.add(w)
                    if pt is None or t not in pt:
                        pre.add(w)
            for t in op.writes:
                w = last_w.get(t)
                if w is not None:
                    deps.add(w)
                    if pt is None or t not in pt:
                        pre.add(w)
                for r in readers.get(t, ()):
                    if not (r.dma_key is None and op.dma_key is None and r.eng == op.eng
                            and (r.eng == "pe" or not self.same_engine_sync)):
                        deps.add(r)
                        if pt is None or t not in pt:
                            pre.add(r)
            op._deps = deps
            op._post = deps - pre
            for t in op.writes:
                last_w[t] = op
                readers[t] = []
            for t in op.reads:
                lst = readers.setdefault(t, [])
                lst[:] = [r for r in lst if not _same_stream(r, op)]
                lst.append(op)
            last_stream[("dma", op.dma_key) if op.dma_key is not None else ("eng", op.eng)] = op
        ops = [o for o in self.ops if not o.barrier]
        for op in ops:
            keep = []
            for d in op._deps:
                if d is op:
                    continue
                if d.dma_key is None and op.dma_key is None and d.eng == op.eng:
                    if d.eng == "pe" or not self.same_engine_sync:
                        continue
                keep.append(d)
            op._deps = keep
            for d in keep:
                d.signal = True
        cnt = {}
        tot = {}
        for op in ops:
            if op.dma_key is not None:
                base = ("dma", op.dma_key)
                step = 16
            elif op.signal:
                base = ("eng", op.eng)
                step = 1
            else:
                continue
            op.signal = True
            n = tot.get(base, 0)
            gen = (n * step) // GEN
            tot[base] = n + 1
            k = base + (gen,)
            cnt[k] = cnt.get(k, 0) + step
            op.token = (k, cnt[k])
        self.sem_keys = list(cnt.keys())
        seen = {e: {} for e in self.ENGS}
        for op in ops:
            need = {}
            needp = {}
            for d in op._deps:
                k, v = d.token
                if seen[op.eng].get(k, 0) >= v:
                    continue
                tgt = needp if (d in op._post) else need
                if tgt.get(k, 0) < v:
                    tgt[k] = v
            for k, v in list(needp.items()):
                if need.get(k, 0) >= v:
                    del needp[k]
            for k, v in list(need.items()) + list(needp.items()):
                if seen[op.eng].get(k, 0) < v:
                    seen[op.eng][k] = v
            op.waits = list(need.items())
            op.waits_post = list(needp.items())
        self.final_counts = cnt
        self.ops = ops

    def emit(self, nc):
        self.analyze()
        with ExitStack() as es:
            sems = {}
            for i, k in enumerate(self.sem_keys):
                sems[k] = es.enter_context(nc.semaphore("s%d" % i))
            block = es.enter_context(nc.Block())
            by_eng = {e: [op for op in self.ops if op.eng == e] for e in self.ENGS}

            def run(e, ops, last=False):
                for op in ops:
                    for k, v in op.waits:
                        e.wait_ge(sems[k], v)
                    for k, v in op.waits_post[1:]:
                        e.wait_ge(sems[k], v)
                    ins = op.fn(e)
                    if op.waits_post:
                        k, v = op.waits_post[0]
                        ins._wait_ge(sems[k], v)
                    if op.signal:
                        k, v = op.token
                        ins.then_inc(sems[k], 16 if k[0] == "dma" else 1)
                if last:
                    for k, v in self.final_counts.items():
                        if k[0] == "dma":
                            e.wait_ge(sems[k], v)

            @block.tensor
            def _(e):
                run(e, by_eng["pe"])

            @block.scalar
            def _(e):
                run(e, by_eng["act"])

            @block.vector
            def _(e):
                run(e, by_eng["dve"])

            @block.gpsimd
            def _(e):
                run(e, by_eng["pool"])

            @block.sync
            def _(e):
                run(e, by_eng["sp"], last=True)


class SB:
    cnt = 0

    def __init__(self, nc, limit=150016, base=16640):
        self.nc = nc
        self.off = base
        self.limit = limit
        self.n = 0

    def alloc(self, name, shape, dtype):
        esz = 4 if dtype in (F32, I32) else 2
        nbytes = esz
        for s in shape[1:]:
            nbytes *= s
        nbytes = (nbytes + 63) // 64 * 64
        assert self.off + nbytes <= self.limit, ("SBUF overflow", name, self.off, nbytes)
        SB.cnt += 1
        t = self.nc.alloc_sbuf_tensor_at("%s_%d" % (name, SB.cnt), list(shape), dtype, offset=self.off)
        self.off += nbytes
        return t

    def mark(self):
        return self.off

    def reset(self, m):
        self.off = m


class Builder:
    def __init__(self, nc, n_layers, taps=(), stop_after=None):
        self.nc = nc
        self.S = Sched()
        self.sb = SB(nc)
        self.sbt = SB(nc, limit=215552, base=150016)
        self.L = n_layers
        self.taps = set(taps)
        self.stop_after = stop_after
        self.uid = 0
        self.keymap = {}

    def dma(self, eng, out, in_, r, w, key):
        km = self.keymap.setdefault(eng, {})
        key = km.setdefault(key, "%s%d" % (eng, len(km)))
        self.S.add(eng, lambda e: e.dma_start(out=out, in_=in_), r, w, dma_key=key)

    def barrier(self):
        self.S.barrier()
        self.keymap = {}

    def mm(self, out, lhsT, rhs, start, stop, r, w, skip=False, post=None):
        if skip:
            op = self.S.add("pe", lambda e: e.matmul(out, lhsT, rhs, start=start, stop=stop, skip_group_check=True), r, w)
        else:
            op = self.S.add("pe", lambda e: e.matmul(out, lhsT, rhs, start=start, stop=stop), r, w)
        if post is not None and EMBED:
            op.post_tags = set(post)

    def tr(self, out, in_, ident, r, w):
        self.S.add("pe", lambda e: e.transpose(out, in_, ident), r, w)

    def act(self, out, in_, func, r, w, **kw):
        self.S.add("act", lambda e: e.activation(out=out, in_=in_, func=func, **kw), r, w)

    def ts(self, eng, out, in0, s1, s2, op0, op1, r, w):
        if s2 is None:
            self.S.add(eng, lambda e: e.tensor_scalar(out=out, in0=in0, scalar1=s1, scalar2=None, op0=op0), r, w)
        else:
            self.S.add(eng, lambda e: e.tensor_scalar(out=out, in0=in0, scalar1=s1, scalar2=s2, op0=op0, op1=op1), r, w)

    def stt(self, eng, out, in0, scalar, in1, op0, op1, r, w):
        self.S.add(eng, lambda e: e.scalar_tensor_tensor(out=out, in0=in0, scalar=scalar, in1=in1, op0=op0, op1=op1), r, w)

    def tt(self, eng, out, in0, in1, op, r, w):
        self.S.add(eng, lambda e: e.tensor_tensor(out=out, in0=in0, in1=in1, op=op), r, w)

    def cp(self, eng, out, in_, r, w):
        if eng == "act":
            self.S.add("act", lambda e: e.activation(out=out, in_=in_, func=AF.Copy), r, w)
        else:
            self.S.add(eng, lambda e: e.tensor_copy(out=out, in_=in_), r, w)

    def memset(self, eng, out, val, r, w):
        self.S.add(eng, lambda e: e.memset(out, val), r, w)

    def recip(self, out, in_, r, w):
        self.S.add("dve", lambda e: e.reciprocal(out=out, in_=in_), r, w)

    def rstd(self, ss, tmp, out, n, tag):
        self.ts("dve", tmp, ss, 1.0 / n, EPS, ALU.mult, ALU.add, [tag + "ss"], [tag + "ms"])
        self.act(tmp, tmp, AF.Sqrt, [tag + "ms"], [tag + "ms"])
        self.recip(out, tmp, [tag + "ms"], [tag + "rstd"])

    def dram(self, name, shape, dtype):
        kind = "ExternalOutput" if name in self.taps else "Internal"
        return self.nc.dram_tensor(name, list(shape), dtype, kind=kind).ap()

    def declare(self):
        nc = self.nc
        ein = lambda n, s, d=F32: nc.dram_tensor(n, list(s), d, kind="ExternalInput").ap()
        self.x = ein("x", [S_, D])
        self.pos = ein("pos", [1, S_], I32)
        self.pre_mix_g = ein("pre_mix_g", [DEPTH, D])
        self.w_in = ein("w_in", [DEPTH, D, IN_COLS])
        self.diff_lambda = ein("diff_lambda", [DEPTH, 256])
        self.diff_head_g = ein("diff_head_g", [DEPTH, 128])
        self.w_a_out = ein("w_a_out", [DEPTH, 512, D])
        self.w_b_out = ein("w_b_out", [DEPTH, 512, D])
        self.w_mix_out = ein("w_mix_out", [DEPTH, D, D])
        self.post_mix_g = ein("post_mix_g", [DEPTH, D])
        self.pre_ffn_g = ein("pre_ffn_g", [DEPTH, D])
        self.w_up = ein("w_up", [DEPTH, D, 2 * D_FF])
        self.conv_wh = ein("conv_wh", [DEPTH, 128, 2 * NFF * 3])
        self.conv_bh = ein("conv_bh", [DEPTH, 128, 2 * NFF])
        self.w_down = ein("w_down", [DEPTH, D_FF, D])
        self.post_ffn_g = ein("post_ffn_g", [DEPTH, D])
        self.cbf_d = ein("cbf", [128, 1024])
        self.crope_d = ein("crope", [128, 4])
        self.out = nc.dram_tensor("out", [S_, D], F32, kind="ExternalOutput").ap()
        self.xr = self.dram("xr", [S_, D], F32)
        self.ropeA = self.dram("ropeA", [2, 128, S_], F32)
        self.ropeB = self.dram("ropeB", [2, 128, S_], F32)
        self.qaT = self.dram("qaT", [512, S_], BF16)
        self.kaT = self.dram("kaT", [512, S_], BF16)
        self.qbT = self.dram("qbT", [1536, S_], BF16)
        self.kbT = self.dram("kbT", [1536, S_], BF16)
        self.va = self.dram("va", [S_, 512], BF16)
        self.vb = self.dram("vb", [S_, 1536], BF16)
        self.gT = self.dram("gT", [2048, S_], BF16)
        self.oaT = self.dram("oaT", [512, S_], BF16)
        self.obT = self.dram("obT", [512, S_], BF16)
        self.aT = self.dram("aT", [D_FF, S_], BF16)

    def consts(self):
        sb = self.sb
        self.cbf = sb.alloc("cbf", [128, 1024], BF16)
        self.crope = sb.alloc("crope", [128, 4], F32)
        self.dma("pool", self.cbf[:], self.cbf_d, [], ["cbf"], "cbf")
        self.dma("sp", self.crope[:], self.crope_d, [], ["crope"], "crope")
        self.ident = self.cbf[:, 0:128]
        self.permA = self.cbf[:, 128:256]
        self.permB = self.cbf[:, 256:384]
        self.maskU = self.cbf[:, 384:512]
        self.maskUL = self.cbf[:, 384:640]
        self.ones = self.cbf[:, 640:768]
        self.negU = self.cbf[:, 768:896]
        self.negUL = self.cbf[:, 768:1024]
        self.ps = self.nc.alloc_psum_tensor("psall", [128, 4096], F32)
        self.psb = self.ps.bitcast(BF16)
        self.barrier()

    def bank(self, i, n=512, off=0):
        return self.ps[:, i * 512 + off:i * 512 + off + n]

    def bank_bf(self, i, n=1024, off=0):
        return self.psb[:, i * 1024 + off:i * 1024 + off + n]

    def phase0(self):
        sb = self.sb
        m = sb.mark()
        posi = sb.alloc("posi", [128, S_], I32)
        posf = sb.alloc("posf", [128, S_], F32)
        ang = sb.alloc("ang", [128, S_], F32)
        kk = sb.alloc("kk", [128, S_], I32)
        kf = sb.alloc("kf", [128, S_], F32)
        rr = sb.alloc("rr", [128, S_], F32)
        tab = [sb.alloc("tab%d" % i, [128, S_], F32) for i in range(2)]
        self.dma("sp", posi[:], self.pos.broadcast_to([128, S_]), [], ["posi"], "posi")
        self.cp("dve", posf[:], posi[:], ["posi"], ["posf"])
        n = 0
        for ti, dst in enumerate((self.ropeA, self.ropeB)):
            self.ts("dve", ang[:], posf[:], self.crope[:, ti:ti + 1], None, ALU.mult, None, ["posf"], ["ang"])
            for which in range(2):
                shift = math.pi / 2 if which == 0 else 0.0
                tb = tab[n % 2]
                tg = "tab%d" % (n % 2)
                n += 1
                self.ts("dve", rr[:], ang[:], shift, None, ALU.add, None, ["ang"], ["rr"])
                self.ts("dve", kk[:], rr[:], 1.0 / TWO_PI, None, ALU.mult, None, ["rr"], ["kk"])
                self.cp("dve", kf[:], kk[:], ["kk"], ["kf"])
                self.stt("dve", rr[:], kf[:], -C1, rr[:], ALU.mult, ALU.add, ["kf", "rr"], ["rr"])
                self.stt("dve", rr[:], kf[:], -C2, rr[:], ALU.mult, ALU.add, ["kf", "rr"], ["rr"])
                self.ts("dve", rr[:], rr[:], 3.1415925, -3.1415925, ALU.min, ALU.max, ["rr"], ["rr"])
                self.act(tb[:], rr[:], AF.Sin, ["rr"], [tg])
                if which == 1:
                    self.ts("pool", tb[:], tb[:], self.crope[:, 2 + ti:3 + ti], None, ALU.mult, None, [tg], [tg])
                self.dma("sp", dst[which], tb[:], [tg], [], tg)
        self.barrier()
        sb.reset(m)

    def build_hT(self, l, xsrc, gvec, hT, tagp):
        sb = self.sb
        g_b = sb.alloc("g_b", [128, D], F32)
        NS = 4
        xt = [sb.alloc("xt%d" % i, [128, D], F32) for i in range(NS)]
        hb = [sb.alloc("hb%d" % i, [128, D], BF16) for i in range(NS)]
        junk = sb.alloc("junk", [128, D], BF16)
        sm = sb.alloc("sm", [128, 3 * NS], F32)
        self.dma("sp", g_b[:], gvec[l:l + 1, :].broadcast_to([128, D]), [], ["g_b"], tagp + "g_b")

        def front(t):
            s = t % NS
            T = tagp + "%d" % s
            self.dma("sp", xt[s][:], xsrc[t * 128:(t + 1) * 128, :], [], [T + "xt"], T + "xt")
            self.act(junk[:], xt[s][:], AF.Square, [T + "xt"], [T + "ss"], accum_out=sm[:, s:s + 1])
            self.rstd(sm[:, s:s + 1], sm[:, NS + s:NS + s + 1], sm[:, 2 * NS + s:2 * NS + s + 1], D, T)
            self.stt("dve", hb[s][:], xt[s][:], sm[:, 2 * NS + s:2 * NS + s + 1], g_b[:], ALU.mult, ALU.mult,
                     [T + "xt", T + "rstd", "g_b"], [T + "hb"])

        def back(t):
            s = t % NS
            T = tagp + "%d" % s
            pst = self.bank_bf(s)
            for kc in range(8):
                self.tr(pst[:, kc * 128:(kc + 1) * 128], hb[s][:, kc * 128:(kc + 1) * 128], self.ident,
                        [T + "hb"], [T + "pst"])
            self.cp("act" if t % 2 == 0 else "dve", hT[:, :, t * 128:(t + 1) * 128],
                    pst.rearrange("p (k t) -> p k t", k=8), [T + "pst"], [])

        front(0)
        front(1)
        for t in range(NT):
            if t + 2 < NT:
                front(t + 2)
            back(t)

    def p1(self, l, xsrc):
        sb = self.sb
        m = sb.mark()
        hT = sb.alloc("hT", [128, 8, S_], BF16)
        m2 = sb.mark()
        self.sbt.reset(150016)
        rope = [self.sbt.alloc("ropeA", [128, 2, S_], F32), self.sbt.alloc("ropeB", [128, 2, S_], F32)]
        for i, src in enumerate((self.ropeA, self.ropeB)):
            for w in range(2):
                self.dma("sp", rope[i][:, w, :], src[w], [], ["rope%d%d" % (i, w)], "rope%d%d" % (i, w))
        self.build_hT(l, xsrc, self.pre_mix_g, hT, "p1a")
        self.barrier()
        sb.reset(m2)
        if self.stop_after == "p1a":
            dbg = self.dram("hTd", [128, 8 * S_], BF16)
            self.dma("sp", dbg, hT[:].rearrange("p k t -> p (k t)"), [], [], "dbg")
            self.barrier()
            sb.reset(m)
            return
        wblk = [sb.alloc("wblk%d" % i, [128, 8, 512], BF16) for i in range(2)]
        tb = [sb.alloc("tb%d" % i, [128, 512], BF16) for i in range(2)]
        tmp = [sb.alloc("tmp%d" % i, [128, 512], F32) for i in range(2)]
        t2 = [sb.alloc("t2%d" % i, [128, 512], F32) for i in range(2)]
        ot = [sb.alloc("ot%d" % i, [128, 512], BF16) for i in range(4)]
        w_l = self.w_in[l].rearrange("(kc p) c -> p kc c", p=128)
        blocks = []
        blocks.append((0, "ra", self.qaT, 0))
        blocks.append((512, "ra", self.kaT, 0))
        blocks.append((1024, "v", self.va, 0))
        for i in range(3):
            blocks.append((1536 + 512 * i, "rb", self.qbT, 512 * i))
        for i in range(3):
            blocks.append((3072 + 512 * i, "rb", self.kbT, 512 * i))
        for i in range(3):
            blocks.append((4608 + 512 * i, "v", self.vb, 512 * i))
        for i in range(4):
            blocks.append((6144 + 512 * i, "g", self.gT, 512 * i))
        it = 0
        io = 0
        if self.stop_after == "p1w":
            self.dma("pool", wblk[0][:], w_l[:, :, 0:512], [], ["wblk0"], "wblk0")
            dbg = self.dram("wd", [128, 8 * 512], BF16)
            self.dma("sp", dbg, wblk[0][:].rearrange("p k t -> p (k t)"), ["wblk0"], [], "dbg")
            dbg2 = self.dram("rd", [128, 2 * S_], F32)
            self.dma("sp", dbg2, rope[1][:].rearrange("p k t -> p (k t)"), ["rope10", "rope11"], [], "dbg2")
            self.barrier()
            sb.reset(m)
            return
        import os
        dbg_nb = int(os.environ.get("K_DBG_NB", "99"))
        dbg_ns = int(os.environ.get("K_DBG_NS", "4"))
        dbg_sel = os.environ.get("K_DBG_SEL", "")
        if dbg_sel:
            blocks = [blocks[int(c)] for c in dbg_sel.split(",")]
        blocks = blocks[:dbg_nb]

        def wload(bi):
            c0 = blocks[bi][0]
            self.dma("pool", wblk[bi % 2][:], w_l[:, :, c0:c0 + 512], [], ["wblk%d" % (bi % 2)], "wblk%d" % (bi % 2))

        wload(0)
        for bi, (c0, kind, dst, d0) in enumerate(blocks):
            s = bi % 2
            W = "wblk%d" % s
            if bi + 1 < len(blocks):
                wload(bi + 1)
            if kind == "v":
                items = [(0, tt) for tt in range(NT)]
            else:
                items = [(sub, tc) for sub in range(dbg_ns) for tc in range(8)]
            base = it

            def G(n, s=s, W=W, kind=kind, items=items, base=base):
                pi = (base + n) % 2
                ps = self.bank(pi)
                sub, tc = items[n]
                for kc in range(8):
                    if kind == "v":
                        self.mm(ps, hT[:, kc, tc * 128:(tc + 1) * 128], wblk[s][:, kc, :], kc == 0, kc == 7, [W], ["psA%d" % pi])
                    else:
                        self.mm(ps, wblk[s][:, kc, sub * 128:(sub + 1) * 128], hT[:, kc, tc * 512:(tc + 1) * 512],
                                kc == 0, kc == 7, [W], ["psA%d" % pi])

            def post(n, kind=kind, items=items, base=base, dst=dst, d0=d0):
                nonlocal io
                pi = (base + n) % 2
                P = "psA%d" % pi
                ps = self.bank(pi)
                sub, tc = items[n]
                o = io % 4
                io += 1
                O = "ot%d" % o
                if kind == "v":
                    self.cp("dve" if tc % 2 == 0 else "act", ot[o][:], ps, [P], [O])
                    self.dma("sp", dst[tc * 128:(tc + 1) * 128, d0:d0 + 512], ot[o][:], [O], [], O)
                    return
                tsl = slice(tc * 512, (tc + 1) * 512)
                if kind == "g":
                    self.act(ot[o][:], ps, AF.Sigmoid, [P], [O])
                else:
                    ri = 0 if kind == "ra" else 1
                    perm = self.permA if kind == "ra" else self.permB
                    P2 = "psB%d" % pi
                    ps2 = self.bank(2 + pi)
                    self.mm(ps2, perm, tb[pi][:], True, True, ["tb%d" % pi], [P2])
                    self.tt("dve", tmp[pi][:], ps, rope[ri][:, 0, tsl], ALU.mult, [P, "rope%d0" % ri], ["tmp%d" % pi, P])
                    self.tt("dve", t2[pi][:], ps2, rope[ri][:, 1, tsl], ALU.mult, [P2, "rope%d1" % ri], ["t2%d" % pi])
                    self.tt("pool", ot[o][:], tmp[pi][:], t2[pi][:], ALU.add, ["tmp%d" % pi, "t2%d" % pi], [O])
                r0 = d0 + sub * 128
                self.dma("sp", dst[r0:r0 + 128, tsl], ot[o][:], [O], [], O)

            def pre(n, kind=kind, base=base):
                if kind in ("ra", "rb"):
                    pi = (base + n) % 2
                    self.cp("act", tb[pi][:], self.bank(pi), ["psA%d" % pi], ["tb%d" % pi])

            G(0)
            for n in range(len(items)):
                pre(n)
                if n + 1 < len(items):
                    G(n + 1)
                post(n)
            it += len(items)
        self.barrier()
        sb.reset(m)

    def lam_setup(self, l):
        sb = self.sb
        lam_init = 0.8 - 0.6 * math.exp(-0.3 * l)
        lv = sb.alloc("lv", [128, 256], F32)
        pr = sb.alloc("pr", [128, 128], F32)
        sc = sb.alloc("lsc", [128, 8], F32)
        gh = sb.alloc("gh", [128, 128], F32)
        self.dma("sp", lv[:], self.diff_lambda[l:l + 1, :].broadcast_to([128, 256]), [], ["lv"], "lv")
        self.dma("sp", gh[:], self.diff_head_g[l:l + 1, :].broadcast_to([128, 128]), [], ["gh"], "gh")
        self.tt("dve", pr[:, 0:64], lv[:, 0:64], lv[:, 64:128], ALU.mult, ["lv"], ["pr"])
        self.tt("dve", pr[:, 64:128], lv[:, 128:192], lv[:, 192:256], ALU.mult, ["lv", "pr"], ["pr"])
        self.S.add("dve", lambda e: e.reduce_sum(out=sc[:, 0:1], in_=pr[:, 0:64], axis=mybir.AxisListType.X), ["pr"], ["sc"])
        self.S.add("dve", lambda e: e.reduce_sum(out=sc[:, 1:2], in_=pr[:, 64:128], axis=mybir.AxisListType.X), ["pr", "sc"], ["sc"])
        self.act(sc[:, 2:4], sc[:, 0:2], AF.Exp, ["sc"], ["sc2"])
        self.stt("dve", sc[:, 4:5], sc[:, 3:4], -lam_init, sc[:, 2:3], ALU.add, ALU.subtract, ["sc2"], ["neglam"])
        self.ts("dve", gh[:], gh[:], 1.0 - lam_init, None, ALU.mult, None, ["gh"], ["gh"])
        self.neglam = sc[:, 4:5]
        self.gh = gh

    def phA(self, l):
        sb = self.sb
        m = sb.mark()
        self.lam_setup(l)
        lam_init = 0.8 - 0.6 * math.exp(-0.3 * l)
        qz = [[sb.alloc("qz%d%d" % (i, c), [128, S_], BF16) for c in range(2)] for i in range(2)]
        kT = [sb.alloc("kT%d" % i, [128, S_], BF16) for i in range(2)]
        V = [sb.alloc("V%d" % i, [128, NT, 128], BF16) for i in range(2)]
        for i in range(2):
            self.memset("pool", qz[i][0][64:128, :], 0.0, [], ["qz%d0" % i])
            self.memset("pool", qz[i][1][0:64, :], 0.0, [], ["qz%d1" % i])
        NPT = 6
        pT = [sb.alloc("pT%d" % i, [128, 512], BF16) for i in range(NPT)]
        osb = [[sb.alloc("osb%d%d" % (f, c), [128, 512], F32) for c in range(2)] for f in range(2)]
        Dn = [[sb.alloc("Dn%d%d" % (f, c), [128, 512], F32) for c in range(2)] for f in range(2)]
        o32 = [sb.alloc("o32%d" % i, [128, 512], F32) for i in range(2)]
        o2 = [sb.alloc("o2%d" % i, [128, 512], F32) for i in range(2)]
        sq = [sb.alloc("sq%d" % i, [128, 512], F32) for i in range(2)]
        s2 = [sb.alloc("s2%d" % i, [128, 512], F32) for i in range(2)]
        rs = [sb.alloc("rs%d" % i, [128, 512], F32) for i in range(2)]
        oT = [sb.alloc("oT%d" % i, [128, 512], BF16) for i in range(2)]
        ones32 = sb.alloc("ones32", [128, 128], F32)
        ghc = sb.alloc("ghc", [128, 1], F32)
        self.memset("pool", ones32[:], 1.0, [], ["ones32"])
        self.dma("sp", ghc[:], self.diff_head_g[l:l + 1, :].rearrange("o d -> d o"), [], ["ghc"], "ghc")
        self.ts("dve", ghc[:], ghc[:], 1.0 - lam_init, None, ALU.mult, None, ["ghc"], ["ghc"])

        def load(h):
            s = h % 2
            for c in range(2):
                self.dma("sp", qz[s][c][c * 64:(c + 1) * 64, :], self.qaT[h * 128 + c * 64:h * 128 + (c + 1) * 64, :],
                         ["qz%d%d" % (s, c)], ["qz%d%d" % (s, c)], "qz%d%d" % (s, c))
            self.dma("sp", kT[s][:], self.kaT[h * 128:(h + 1) * 128, :], [], ["kT%d" % s], "kT%d" % s)
            self.dma("sp", V[s][:], self.va[:, h * 128:(h + 1) * 128].rearrange("(t p) d -> p t d", p=128),
                     [], ["V%d" % s], "V%d" % s)

        load(0)
        ip = 0
        pending = []
        for h in range(4):
            s = h % 2
            if h + 1 < 4:
                load(h + 1)
            items = []
            for j in range(8):
                for c in range(2):
                    nkt = 4 * j + 4
                    for kt in range(nkt):
                        q0 = max(j * 512, kt * 128)
                        items.append((j, c, kt, q0, (j + 1) * 512 - q0, nkt))
            base = ip

            def qk(n):
                j, c, kt, q0, N, nkt = items[n]
                i = base + n
                P = "psS%d" % (i % 3)
                diag = kt >= 4 * j
                self.mm(self.bank(i % 3, N), kT[s][:, kt * 128:(kt + 1) * 128],
                        qz[s][c][:, q0:q0 + N], True, not diag, ["kT%d" % s, "qz%d%d" % (s, c)], [P], post=[P])
                if diag:
                    self.mm(self.bank(i % 3, 128), self.ident, self.negU, False, True, [], [P])

            def ex(n):
                j, c, kt, q0, N, nkt = items[n]
                i = base + n
                self.act(pT[i % NPT][:, 0:N], self.bank(i % 3, N), AF.Exp, ["psS%d" % (i % 3)], ["pT%d" % (i % NPT)], scale=0.125)

            def pv(n):
                j, c, kt, q0, N, nkt = items[n]
                i = base + n
                T = "pT%d" % (i % NPT)
                off = q0 - j * 512
                PO = "psO%d" % c
                PDN = "psDn%d" % c
                self.mm(self.bank(3 + c)[:, off:off + N], V[s][:, kt, :], pT[i % NPT][:, 0:N], kt == 0, kt == nkt - 1,
                        [T, "V%d" % s], [PO], post=[T, PO])
                self.mm(self.bank(5 + c)[:, off:off + N], self.ones, pT[i % NPT][:, 0:N], kt == 0, kt == nkt - 1,
                        [T], [PDN], post=[T, PDN])

            def tail(n):
                nonlocal pending
                j, c, kt, q0, N, nkt = items[n]
                f = (h * 8 + j) % 2
                Tg = "fA%d" % f
                if c == 1 and kt == min(7, nkt - 1) and pending:
                    for fn in pending:
                        fn()
                    pending = []
                if kt != nkt - 1:
                    return
                self.cp("dve", osb[f][c][:], self.bank(3 + c), ["psO%d" % c], [Tg + "os%d" % c, "psO%d" % c])
                self.cp("dve", Dn[f][c][:], self.bank(5 + c), ["psDn%d" % c], [Tg + "D%d" % c, "psDn%d" % c])
                if c == 0:
                    return
                self.tt("dve", o32[f][:], osb[f][0][:], Dn[f][1][:], ALU.mult, [Tg + "os0", Tg + "D1"], [Tg + "o"])
                self.tt("pool", o2[f][:], osb[f][1][:], Dn[f][0][:], ALU.mult, [Tg + "os1", Tg + "D0"], [Tg + "o2"])
                self.stt("dve", o32[f][:], o2[f][:], self.neglam, o32[f][:], ALU.mult, ALU.add, [Tg + "o", Tg + "o2", "neglam"], [Tg + "o"])
                self.tt("pool", sq[f][:], o32[f][:], o32[f][:], ALU.mult, [Tg + "o"], [Tg + "sq"])
                self.tt("pool", s2[f][:], Dn[f][0][:], Dn[f][1][:], ALU.mult, [Tg + "D0", Tg + "D1"], [Tg + "s2"])
                self.tt("pool", s2[f][:], s2[f][:], s2[f][:], ALU.mult, [Tg + "s2"], [Tg + "s2"])

                def stageB(f=f, Tg=Tg, h=h, j=j):
                    self.mm(self.bank(7), ones32[:], sq[f][:], True, True, [Tg + "sq", "ones32"], ["psQ"])
                    self.stt("dve", rs[f][:], s2[f][:], 128.0 * EPS, self.bank(7), ALU.mult, ALU.add, ["psQ", Tg + "s2"], [Tg + "rs", "psQ"])
                    self.act(rs[f][:], rs[f][:], AF.Ln, [Tg + "rs"], [Tg + "rs"], scale=1.0 / 128)
                    self.act(rs[f][:], rs[f][:], AF.Exp, [Tg + "rs"], [Tg + "rs"], scale=-0.5)
                    self.stt("dve", oT[f][:], o32[f][:], ghc[:], rs[f][:], ALU.mult, ALU.mult, [Tg + "o", "ghc", Tg + "rs"], ["oT%d" % f])
                    self.dma("sp", self.oaT[h * 128:(h + 1) * 128, j * 512:(j + 1) * 512], oT[f][:], ["oT%d" % f], [], "oT%d" % f)

                pending.append(stageB)

            qk(0)
            qk(1)
            for n in range(len(items)):
                if n + 2 < len(items):
                    qk(n + 2)
                ex(n)
                pv(n)
                tail(n)
            ip += len(items)
        for fn in pending:
            fn()
        self.barrier()
        sb.reset(m)

    def phB(self, l):
        sb = self.sb
        m = sb.mark()
        self.sbt.reset(150016)
        self.Mw = (self.sbt.alloc("wa", [128, 4, D], BF16), self.sbt.alloc("wb", [128, 4, D], BF16),
                   self.sbt.alloc("wm", [128, 8, D], BF16))
        self.dma("pool", self.Mw[0][:], self.w_a_out[l].rearrange("(kc p) c -> p kc c", p=128), [], ["wa"], "wa")
        self.dma("pool", self.Mw[1][:], self.w_b_out[l].rearrange("(kc p) c -> p kc c", p=128), [], ["wb"], "wb")
        self.dma("pool", self.Mw[2][:], self.w_mix_out[l].rearrange("(kc p) c -> p kc c", p=128), [], ["wm"], "wm")
        qT = [sb.alloc("bqT%d" % i, [128, S_], BF16) for i in range(2)]
        kT = [sb.alloc("bkT%d" % i, [128, S_], BF16) for i in range(2)]
        V = [sb.alloc("bV%d" % i, [128, NT, 128], BF16) for i in range(2)]
        accN_ = [sb.alloc("accN%d" % i, [128, S_], F32) for i in range(2)]
        accD_ = [sb.alloc("accD%d" % i, [128, S_], F32) for i in range(2)]
        pT = [sb.alloc("bpT%d" % i, [128, 256], BF16) for i in range(6)]
        obt = sb.alloc("obt", [128, S_], BF16)
        scale = 128 ** -0.5
        combos = [(h, g) for h in range(4) for g in range(3)]

        def load(ci):
            h, g = combos[ci]
            s = ci % 2
            dil = B_PAIRS[g][1]
            nb = NT // dil
            row = (g * 4 + h) * 128
            self.dma("sp", qT[s][:], self.qbT[row:row + 128, :], [], ["qT%d" % s], "bqT%d" % s)
            self.dma("sp", kT[s][:], self.kbT[row:row + 128, :], [], ["kT%d" % s], "bkT%d" % s)
            vsrc = self.vb[:, row:row + 128]
            for r in range(dil):
                src = vsrc[r:S_:dil, :].rearrange("(b i) d -> i b d", i=128)
                wr = ["V%dr%d" % (s, r)] + (["V%dall" % s] if r == 0 else [])
                self.dma("sp", V[s][:, r * nb:(r + 1) * nb, :], src, [], wr, "bV%d_%d" % (s, r % 4))

        load(0)
        ip = 0
        ig = 0
        for ci, (h, g) in enumerate(combos):
            s = ci % 2
            dil = B_PAIRS[g][1]
            nb = NT // dil
            accN = accN_[h % 2]
            accD = accD_[h % 2]
            AN = "accN%d" % (h % 2)
            AD = "accD%d" % (h % 2)
            if ci + 1 < len(combos):
                load(ci + 1)
            items = [(r, b) for r in range(dil) for b in range(nb)]
            base = ip

            def qk(n):
                r, b = items[n]
                i = base + n
                st = r + dil * 128 * b
                nq = 256 if b + 1 < nb else 128
                kcols = slice(st, st + dil * 127 + 1, dil)
                qcols = slice(st, st + dil * (nq - 1) + 1, dil)
                P = "psS%d" % (i % 3)
                self.mm(self.bank(i % 3, nq), kT[s][:, kcols], qT[s][:, qcols], True, False,
                        ["kT%d" % s, "qT%d" % s], [P])
                self.mm(self.bank(i % 3, nq), self.ident, self.negUL[:, 0:nq], False, True, [], [P])

            def ex(n):
                r, b = items[n]
                i = base + n
                nq = 256 if b + 1 < nb else 128
                self.act(pT[i % 6][:, 0:nq], self.bank(i % 3, nq), AF.Exp, ["psS%d" % (i % 3)], ["pT%d" % (i % 6)], scale=scale)

            state = {"gi": None}

            def pv(n):
                nonlocal ig
                r, b = items[n]
                i = base + n
                T = "pT%d" % (i % 6)
                if b % 4 == 0:
                    state["gi"] = ig % 2
                    ig += 1
                gi = state["gi"]
                psN = self.bank(3 + gi)
                psD = self.bank(5 + gi)
                PN = "psN%d" % gi
                PD = "psD%d" % gi
                n_cur = r * nb + b
                col = (b % 4) * 128
                VT = ["V%dr%d" % (s, r), "V%dall" % s]
                for (pso, PT_, lw) in ((psN, PN, None), (psD, PD, self.ones)):
                    first = True
                    if b > 0:
                        ipv = i - 1
                        lhs = V[s][:, n_cur - 1, :] if lw is None else lw
                        self.mm(pso[:, col:col + 128], lhs, pT[ipv % 6][:, 128:256], True, False,
                                ["pT%d" % (ipv % 6)] + VT, [PT_])
                        first = False
                    lhs = V[s][:, n_cur, :] if lw is None else lw
                    self.mm(pso[:, col:col + 128], lhs, pT[i % 6][:, 0:128], first, True, [T] + VT, [PT_])
                if b % 4 == 3 or b == nb - 1:
                    b0 = (b // 4) * 4
                    nblk = b - b0 + 1
                    st0 = r + dil * 128 * b0
                    tsl = slice(st0, st0 + dil * (128 * nblk - 1) + 1, dil)
                    nn = nblk * 128
                    if g == 0:
                        self.cp("dve", accN[:, tsl], psN[:, 0:nn], [PN], [AN, PN])
                        self.cp("dve", accD[:, tsl], psD[:, 0:nn], [PD], [AD, PD])
                    else:
                        self.tt("dve", accN[:, tsl], accN[:, tsl], psN[:, 0:nn], ALU.add, [PN, AN], [AN, PN])
                        self.tt("dve", accD[:, tsl], accD[:, tsl], psD[:, 0:nn], ALU.add, [PD, AD], [AD, PD])

            qk(0)
            if len(items) > 1:
                qk(1)
            for n in range(len(items)):
                if n + 2 < len(items):
                    qk(n + 2)
                ex(n)
                pv(n)
            ip += len(items)
            if g == 2:
                for q in range(4):
                    qs = slice(q * 1024, (q + 1) * 1024)
                    self.act(accD[:, qs], accD[:, qs], AF.Ln, [AD], [AD])
                    self.act(accD[:, qs], accD[:, qs], AF.Exp, [AD], [AD], scale=-1.0)
                    self.tt("pool", obt[:, qs], accN[:, qs], accD[:, qs], ALU.mult, [AN, AD], ["obt"])
                self.dma("sp", self.obT[h * 128:(h + 1) * 128, :], obt[:], ["obt"], [], "obt")
        self.barrier()
        sb.reset(m)

    def post_norm_res(self, psv, g_b, xtile, dst, k, sm, tmp, xo, tagp, r_extra, w_x):
        f = k % 2
        T = tagp + "%d" % f
        junk = self._pjunk
        self.act(junk[:, 0:512], psv[:, 0:512], AF.Square, r_extra, [T + "ssa"], accum_out=sm[f][:, 0:1])
        self.act(junk[:, 512:1024], psv[:, 512:1024], AF.Square, r_extra, [T + "ssb"], accum_out=sm[f][:, 1:2])
        self.tt("dve", sm[f][:, 2:3], sm[f][:, 0:1], sm[f][:, 1:2], ALU.add, [T + "ssa", T + "ssb"], [T + "ss"])
        self.rstd(sm[f][:, 2:3], sm[f][:, 3:4], sm[f][:, 4:5], D, T)
        self.stt("dve", tmp[f][:], psv, sm[f][:, 4:5], g_b[:], ALU.mult, ALU.mult, r_extra + [T + "rstd", tagp + "g_b"], [T + "tmp"])
        self.tt("pool", xo[f][:], tmp[f][:], xtile, ALU.add, [T + "tmp"] + w_x, [T + "xo"])
        self.dma("sp", dst, xo[f][:], [T + "xo"], [], T + "xo")

    def phM(self, l, xsrc, xdst):
        sb = self.sb
        m = sb.mark()
        wa, wb, wm = self.Mw
        g_b = sb.alloc("g_bM", [128, D], F32)
        self.dma("sp", g_b[:], self.post_mix_g[l:l + 1, :].broadcast_to([128, D]), [], ["Mg_b"], "Mg_b")
        oac = [sb.alloc("oac%d" % i, [128, 4, 512], BF16) for i in range(2)]
        obc = [sb.alloc("obc%d" % i, [128, 4, 512], BF16) for i in range(2)]
        gc = [sb.alloc("gc%d" % i, [128, 16, 512], BF16) for i in range(2)]
        xc = [sb.alloc("xc%d" % i, [128, 4, D], F32) for i in range(2)]
        mT = sb.alloc("mT", [128, 8, 512], BF16)
        ta = [sb.alloc("ta%d" % i, [128, 512], F32) for i in range(2)]
        tb_ = [sb.alloc("tbm%d" % i, [128, 512], F32) for i in range(2)]
        sm = [sb.alloc("smM%d" % i, [128, 8], F32) for i in range(2)]
        tmp = [sb.alloc("tmpM%d" % i, [128, D], F32) for i in range(2)]
        xo = [sb.alloc("xoM%d" % i, [128, D], F32) for i in range(2)]
        self._pjunk = sb.alloc("pjunk", [128, D], BF16)

        def load(j):
            s = j % 2
            tsl = slice(j * 512, (j + 1) * 512)
            self.dma("sp", oac[s][:], self.oaT[:, tsl].rearrange("(kc p) t -> p kc t", p=128), [], ["oac%d" % s], "oac%d" % s)
            self.dma("sp", obc[s][:], self.obT[:, tsl].rearrange("(kc p) t -> p kc t", p=128), [], ["obc%d" % s], "obc%d" % s)
            self.dma("sp", gc[s][:], self.gT[:, tsl].rearrange("(kc p) t -> p kc t", p=128), [], ["gc%d" % s], "gc%d" % s)
            self.dma("sp", xc[s][:], xsrc[tsl, :].rearrange("(t p) d -> p t d", p=128), [], ["xc%d" % s], "xc%d" % s)

        load(0)
        k = 0
        iy = 0
        for j in range(8):
            s = j % 2
            if j + 1 < 8:
                load(j + 1)
            for fb in range(8):
                pi = iy % 2
                iy += 1
                psa = self.bank(pi)
                psb = self.bank(2 + pi)
                fsl = slice(fb * 128, (fb + 1) * 128)
                for kc in range(4):
                    self.mm(psa, wa[:, kc, fsl], oac[s][:, kc, :], kc == 0, kc == 3, ["wa", "oac%d" % s], ["psa%d" % pi])
                for kc in range(4):
                    self.mm(psb, wb[:, kc, fsl], obc[s][:, kc, :], kc == 0, kc == 3, ["wb", "obc%d" % s], ["psb%d" % pi])
                self.tt("dve", ta[pi][:], psa, gc[s][:, fb, :], ALU.mult, ["psa%d" % pi, "gc%d" % s], ["ta%d" % pi])
                self.tt("dve", tb_[pi][:], psb, gc[s][:, 8 + fb, :], ALU.mult, ["psb%d" % pi, "gc%d" % s], ["tb%d" % pi])
                self.tt("pool", mT[:, fb, :], ta[pi][:], tb_[pi][:], ALU.add, ["ta%d" % pi, "tb%d" % pi], ["mT%d" % fb])
            for tt in range(4):
                pm = 4 + 2 * (k % 2)
                PM = "psM%d" % (k % 2)
                psv = self.ps[:, pm * 512:(pm + 2) * 512]
                for half in range(2):
                    for kc in range(8):
                        self.mm(psv[:, half * 512:(half + 1) * 512], mT[:, kc, tt * 128:(tt + 1) * 128],
                                wm[:, kc, half * 512:(half + 1) * 512], kc == 0, kc == 7, ["mT%d" % kc, "wm"], [PM])
                row = j * 512 + tt * 128
                self.post_norm_res(psv, g_b, xc[s][:, tt, :], xdst[row:row + 128, :], k, sm, tmp, xo, "M",
                                   [PM], ["xc%d" % s])
                k += 1
        self.barrier()
        sb.reset(m)

    def phF1(self, l, xsrc):
        sb = self.sb
        m = sb.mark()
        hT = sb.alloc("h2T", [128, 8, S_], BF16)
        m2 = sb.mark()
        self.build_hT(l, xsrc, self.pre_ffn_g, hT, "f0")
        self.barrier()
        sb.reset(m2)
        self.sbt.reset(150016)
        self.wd = self.sbt.alloc("wd", [128, NFF, D], BF16)
        wsrc = self.w_down[l].rearrange("(kc p) c -> p kc c", p=128)
        self.dma("pool", self.wd[:, 0:11, :], wsrc[:, 0:11, :], [], ["wd0"], "wd0")
        self.dma("pool", self.wd[:, 11:22, :], wsrc[:, 11:22, :], [], ["wd1"], "wd1")
        cw = sb.alloc("cw", [128, 2 * NFF * 3], F32)
        cb = sb.alloc("cb", [128, 2 * NFF], F32)
        self.dma("sp", cw[:], self.conv_wh[l], [], ["cw"], "cw")
        self.dma("sp", cb[:], self.conv_bh[l], [], ["cw"], "cb")
        wg = [sb.alloc("wg%d" % i, [128, 8, 128], BF16) for i in range(2)]
        wv = [sb.alloc("wv%d" % i, [128, 8, 128], BF16) for i in range(2)]
        ug = [sb.alloc("ug%d" % i, [128, 514], F32) for i in range(2)]
        uv = [sb.alloc("uv%d" % i, [128, 514], F32) for i in range(2)]
        cg = [sb.alloc("cg%d" % i, [128, 512], F32) for i in range(3)]
        cv = [sb.alloc("cv%d" % i, [128, 512], F32) for i in range(3)]
        gg = [sb.alloc("gg%d" % i, [128, 512], F32) for i in range(2)]
        at = [sb.alloc("at%d" % i, [128, 512], BF16) for i in range(4)]
        w_l = self.w_up[l].rearrange("(kc p) c -> p kc c", p=128)

        def wload(i):
            s = i % 2
            self.dma("pool", wg[s][:], w_l[:, :, i * 128:(i + 1) * 128], [], ["wg%d" % s], "wg%d" % s)
            self.dma("pool", wv[s][:], w_l[:, :, D_FF + i * 128:D_FF + (i + 1) * 128], [], ["wv%d" % s], "wv%d" % s)

        items = [(i, tc) for i in range(NFF) for tc in range(8)]

        def front(it):
            i, tc = items[it]
            s = i % 2
            if tc == 0 and i + 1 < NFF:
                wload(i + 1)
            u = it % 2
            up = (it - 1) % 2
            c3 = it % 3
            tsl = slice(tc * 512, (tc + 1) * 512)
            psg = self.bank(u)
            psv = self.bank(2 + u)
            for kc in range(8):
                self.mm(psg, wg[s][:, kc, :], hT[:, kc, tsl], kc == 0, kc == 7, ["wg%d" % s], ["psg%d" % u])
            for kc in range(8):
                self.mm(psv, wv[s][:, kc, :], hT[:, kc, tsl], kc == 0, kc == 7, ["wv%d" % s], ["psv%d" % u])
            for (nm, ub, psx, cc, blk) in (("g", ug, psg, cg, i), ("v", uv, psv, cv, NFF + i)):
                U = "u%s%d" % (nm, u)
                Up = "u%s%d" % (nm, up)
                PS = "ps%s%d" % (nm, u)
                C = "c%s%d" % (nm, c3)
                w0 = cw[:, blk * 3 + 0:blk * 3 + 1]
                w1 = cw[:, blk * 3 + 1:blk * 3 + 2]
                w2 = cw[:, blk * 3 + 2:blk * 3 + 3]
                bb = cb[:, blk:blk + 1]
                self.cp("act", ub[u][:, 2:514], psx, [PS], [U])
                self.act(cc[c3][:], psx, AF.Identity, [PS, "cw"], [C], scale=w2, bias=bb)
                if tc == 0:
                    self.memset("pool", ub[u][:, 0:2], 0.0, [U], [U])
                else:
                    self.cp("pool", ub[u][:, 0:2], ub[up][:, 512:514], [Up, U], [U])
                self.stt("dve", cc[c3][:], ub[u][:, 1:513], w1, cc[c3][:], ALU.mult, ALU.add, [U, "cw", C], [C])
                self.stt("dve", cc[c3][:], ub[u][:, 0:512], w0, cc[c3][:], ALU.mult, ALU.add, [U, "cw", C], [C])

        def back(it):
            i, tc = items[it]
            u = it % 2
            c3 = it % 3
            tsl = slice(tc * 512, (tc + 1) * 512)
            self.act(gg[u][:], cg[c3][:], AF.Gelu_apprx_tanh, ["cg%d" % c3], ["gg%d" % u])
            o = it % 4
            self.tt("pool", at[o][:], gg[u][:], cv[c3][:], ALU.mult, ["gg%d" % u, "cv%d" % c3], ["at%d" % o])
            self.dma("sp", self.aT[i * 128:(i + 1) * 128, tsl], at[o][:], ["at%d" % o], [], "at%d" % o)

        wload(0)
        front(0)
        for it in range(len(items)):
            if it + 1 < len(items):
                front(it + 1)
            back(it)
        self.barrier()
        sb.reset(m)

    def phF2(self, l, xsrc, xdst):
        sb = self.sb
        m = sb.mark()
        wd = self.wd
        g_b = sb.alloc("g_bF", [128, D], F32)
        self.dma("sp", g_b[:], self.post_ffn_g[l:l + 1, :].broadcast_to([128, D]), [], ["Fg_b"], "Fg_b")
        ac = [sb.alloc("ac%d" % i, [128, NFF, 512], BF16) for i in range(2)]
        xc = [sb.alloc("xcF%d" % i, [128, 4, D], F32) for i in range(2)]
        sm = [sb.alloc("smF%d" % i, [128, 8], F32) for i in range(2)]
        tmp = [sb.alloc("tmpF%d" % i, [128, D], F32) for i in range(2)]
        xo = [sb.alloc("xoF%d" % i, [128, D], F32) for i in range(2)]
        self._pjunk = sb.alloc("pjunkF", [128, D], BF16)

        def load(j):
            s = j % 2
            tsl = slice(j * 512, (j + 1) * 512)
            self.dma("sp", ac[s][:], self.aT[:, tsl].rearrange("(kc p) t -> p kc t", p=128), [], ["ac%d" % s], "ac%d" % s)
            self.dma("sp", xc[s][:], xsrc[tsl, :].rearrange("(t p) d -> p t d", p=128), [], ["xc%d" % s], "xcF%d" % s)

        load(0)
        k = 0
        for j in range(8):
            s = j % 2
            if j + 1 < 8:
                load(j + 1)
            for tt in range(4):
                pm = 2 * (k % 4)
                PM = "psF%d" % (k % 4)
                psv = self.ps[:, pm * 512:(pm + 2) * 512]
                for half in range(2):
                    for kc in range(NFF):
                        self.mm(psv[:, half * 512:(half + 1) * 512], ac[s][:, kc, tt * 128:(tt + 1) * 128],
                                wd[:, kc, half * 512:(half + 1) * 512], kc == 0, kc == NFF - 1, ["ac%d" % s, "wd"], [PM])
                row = j * 512 + tt * 128
                self.post_norm_res(psv, g_b, xc[s][:, tt, :], xdst[row:row + 128, :], k, sm, tmp, xo, "F",
                                   [PM], ["xc%d" % s])
                k += 1
        self.barrier()
        sb.reset(m)

    def build(self):
        self.declare()
        self.consts()
        self.phase0()
        stop = self.stop_after
        for l in range(self.L if stop != "p0" else 0):
            xsrc = self.x if l == 0 else self.xr
            last = (l == self.L - 1)
            self.p1(l, xsrc)
            if stop in ("p1", "p1a", "p1w"):
                break
            self.phA(l)
            if stop == "A":
                break
            self.phB(l)
            if stop == "B":
                break
            self.phM(l, xsrc, self.xr)
            if stop == "M":
                break
            self.phF1(l, self.xr)
            if stop == "F1":
                break
            self.phF2(l, self.xr, self.out if last else self.xr)
        self.S.emit(self.nc)


def host_consts():
    k = np.arange(128)
    ident = np.eye(128, dtype=np.float32)
    pa = np.where((k % 64) < 32, k + 32, k - 32)
    pb = np.where(k < 64, k + 64, k - 64)
    permA = np.zeros((128, 128), np.float32)
    permA[pa, k] = 1.0
    permB = np.zeros((128, 128), np.float32)
    permB[pb, k] = 1.0
    maskU = (k[:, None] <= k[None, :]).astype(np.float32)
    maskL = (k[:, None] >= k[None, :]).astype(np.float32)
    ones = np.ones((128, 128), np.float32)
    NEG = np.float32(-30000.0)
    cbf = np.concatenate([ident, permA, permB, maskU, maskL, ones, (1 - maskU) * NEG, (1 - maskL) * NEG], axis=1)
    invA = (np.float32(10000.0) ** (-(np.arange(0, 64, 2, dtype=np.float32)) / np.float32(64))).astype(np.float32)
    invB = (np.float32(10000.0) ** (-(np.arange(0, 128, 2, dtype=np.float32)) / np.float32(128))).astype(np.float32)
    crope = np.zeros((128, 4), np.float32)
    crope[:, 0] = invA[k % 32]
    crope[:, 1] = invB[k % 64]
    crope[:, 2] = np.where((k % 64) < 32, -1.0, 1.0)
    crope[:, 3] = np.where(k < 64, -1.0, 1.0)
    return np.ascontiguousarray(cbf), crope


def make_in_maps(inputs, cores):
    cbf, crope = host_consts()
    f = lambda a: np.ascontiguousarray(np.asarray(a))
    cw = np.asarray(inputs["conv_w"])
    Ld = cw.shape[0]
    cwh = f(cw.transpose(0, 2, 1).reshape(Ld, 2 * NFF, 128, 3).transpose(0, 2, 1, 3).reshape(Ld, 128, 2 * NFF * 3))
    cbh = f(np.asarray(inputs["conv_b"]).reshape(Ld, 2 * NFF, 128).transpose(0, 2, 1))
    shared = {
        "pre_mix_g": f(inputs["pre_mix_g"]), "w_in": f(inputs["w_in"]),
        "diff_lambda": f(np.asarray(inputs["diff_lambda"]).reshape(Ld, 256)),
        "diff_head_g": f(inputs["diff_head_g"]), "w_a_out": f(inputs["w_a_out"]), "w_b_out": f(inputs["w_b_out"]),
        "w_mix_out": f(inputs["w_mix_out"]), "post_mix_g": f(inputs["post_mix_g"]), "pre_ffn_g": f(inputs["pre_ffn_g"]),
        "w_up": f(inputs["w_up"]), "conv_wh": cwh, "conv_bh": cbh, "w_down": f(inputs["w_down"]),
        "post_ffn_g": f(inputs["post_ffn_g"]), "cbf": cbf, "crope": crope,
    }
    x = np.asarray(inputs["x"])
    pos = np.asarray(inputs["positions"]).astype(np.int32)
    maps = []
    for b in cores:
        mp = dict(shared)
        mp["x"] = f(x[b])
        mp["pos"] = f(pos[b][None, :])
        maps.append(mp)
    return maps


def build_program(n_layers=DEPTH, taps=(), stop_after=None):
    nc = bass.Bass("TRN2", target_bir_lowering=False)
    b = Builder(nc, n_layers, taps, stop_after)
    b.build()
    return nc


def kernel(**inputs):
    nc = build_program()
    maps = make_in_maps(inputs, list(range(8)))
    res = run_bass_kernel_spmd(nc, maps, core_ids=list(range(8)))
    return np.stack([np.asarray(r["out"]) for r in res.results], axis=0).astype(np.float32)
```
